# Optimizing a Trainium2 kernel written in Bass

```python
import math
import jax, jax.numpy as jnp
from jax import lax
import numpy as np

D_MODEL = 1024
BATCH = 2
SEQ = 8192
DEPTH = 1

CHUNK = 128
A_GROUPS = 8
A_GROUP_DIM = 128
D_A = A_GROUPS * A_GROUP_DIM
N_HEADS = 8
Q_RANK = 384
KV_RANK = 256
QK_NOPE = 128
QK_ROPE = 64
V_DIM = 128
QK_DIM = QK_NOPE + QK_ROPE
Q_BLOCK = 128
ROPE_THETA = 10000.0
D_FF = 2816
CONV_W = 3
EPS = 1e-6
N_IN = 2 * D_A + Q_RANK + KV_RANK + QK_ROPE + 2 * D_MODEL

kernel_name = "hybrid_gmlp_mla_gated_encoder_block"


def rms_norm(x, g):
    xf = x.astype(jnp.float32)
    y = xf * lax.rsqrt(jnp.mean(xf * xf, axis=-1, keepdims=True) + EPS)
    return (y * g.astype(jnp.float32)).astype(x.dtype)


def layer_norm(x, g, b):
    xf = x.astype(jnp.float32)
    mu = jnp.mean(xf, axis=-1, keepdims=True)
    xc = xf - mu
    y = xc * lax.rsqrt(jnp.mean(xc * xc, axis=-1, keepdims=True) + EPS)
    return (y * g.astype(jnp.float32) + b.astype(jnp.float32)).astype(x.dtype)


def rope_tables(positions, dtype):
    inv_freq = ROPE_THETA ** (-jnp.arange(0, QK_ROPE, 2, dtype=jnp.float32) / QK_ROPE)
    ang = positions.astype(jnp.float32)[..., None] * inv_freq
    return jnp.cos(ang)[:, :, None, :].astype(dtype), jnp.sin(ang)[:, :, None, :].astype(dtype)


def apply_rope(x, cos, sin):
    x1, x2 = jnp.split(x, 2, axis=-1)
    return jnp.concatenate([x1 * cos - x2 * sin, x2 * cos + x1 * sin], axis=-1)


def gmlp_branch(u, v, v_ln_g, v_ln_b, w_s, b_s, w_a_o):
    B, S, _ = u.shape
    u = jax.nn.gelu(u)
    v = layer_norm(jax.nn.gelu(v), v_ln_g, v_ln_b)
    v = v.reshape(B, S // CHUNK, CHUNK, A_GROUPS, A_GROUP_DIM)
    mixed = jnp.einsum('gpq,bcqgd->bcpgd', w_s, v) + jnp.transpose(b_s)[None, None, :, :, None]
    y = u * mixed.reshape(B, S, D_A)
    return y @ w_a_o


def mla_branch(c_q, c_kv, k_rope, positions, q_norm_g, w_uq, kv_norm_g, w_ukv,
               q_head_g, k_head_g, w_b_o):
    B, S, _ = c_q.shape
    cos, sin = rope_tables(positions, c_q.dtype)
    q = (rms_norm(c_q, q_norm_g) @ w_uq).reshape(B, S, N_HEADS, QK_DIM)
    q = rms_norm(q, q_head_g)
    q = jnp.concatenate([q[..., :QK_NOPE], apply_rope(q[..., QK_NOPE:], cos, sin)], axis=-1)
    kv = (rms_norm(c_kv, kv_norm_g) @ w_ukv).reshape(B, S, N_HEADS, QK_NOPE + V_DIM)
    k_nope, v = kv[..., :QK_NOPE], kv[..., QK_NOPE:]
    k_pe = jnp.broadcast_to(k_rope[:, :, None, :], (B, S, N_HEADS, QK_ROPE))
    k = rms_norm(jnp.concatenate([k_nope, k_pe], axis=-1), k_head_g)
    k = jnp.concatenate([k[..., :QK_NOPE], apply_rope(k[..., QK_NOPE:], cos, sin)], axis=-1)
    scale = 1.0 / math.sqrt(QK_DIM)
    n_blocks = S // Q_BLOCK
    qb = jnp.transpose(q.reshape(B, n_blocks, Q_BLOCK, N_HEADS, QK_DIM), (1, 0, 2, 3, 4))

    def attend(q_blk):
        s = jnp.einsum('bqhd,bkhd->bhqk', q_blk, k).astype(jnp.float32) * scale
        p = jax.nn.softmax(s, axis=-1).astype(v.dtype)
        return jnp.einsum('bhqk,bkhd->bqhd', p, v)

    o = lax.map(attend, qb)
    o = jnp.transpose(o, (1, 0, 2, 3, 4)).reshape(B, S, N_HEADS * V_DIM)
    return o @ w_b_o


def conv_gated_mlp(h, norm2_g, w_up, conv_w, conv_b, w_down):
    C = 2 * D_FF
    up = rms_norm(h, norm2_g) @ w_up
    up = lax.conv_general_dilated(
        up, conv_w.reshape(CONV_W, 1, C).astype(up.dtype), window_strides=(1,), padding='SAME',
        dimension_numbers=('NWC', 'WIO', 'NWC'), feature_group_count=C) + conv_b
    val, gate = up[..., :D_FF], up[..., D_FF:]
    return (jax.nn.silu(gate) * val) @ w_down


def setup_inputs(seed: int = 0) -> dict:
    key = jax.random.key(seed)
    ks = jax.random.split(key, 24)
    f32 = jnp.float32

    def w(k, shape, fan_in, mult=1.0):
        return jax.random.normal(k, shape, f32) * (mult * fan_in ** -0.5)

    def gain(k, n):
        return 1.0 + 0.02 * jax.random.normal(k, (n,), f32)

    positions = jnp.broadcast_to(jnp.arange(SEQ, dtype=jnp.int32)[None, :], (BATCH, SEQ))
    return {
        "x": jax.random.normal(ks[0], (BATCH, SEQ, D_MODEL), f32),
        "positions": positions,
        "norm1_g": gain(ks[1], D_MODEL),
        "w_in": w(ks[2], (D_MODEL, N_IN), D_MODEL),
        "v_ln_g": gain(ks[3], D_A),
        "v_ln_b": 0.02 * jax.random.normal(ks[4], (D_A,), f32),
        "w_s": w(ks[5], (A_GROUPS, CHUNK, CHUNK), CHUNK),
        "b_s": 1.0 + 0.02 * jax.random.normal(ks[6], (A_GROUPS, CHUNK), f32),
        "w_a_o": w(ks[7], (D_A, D_MODEL), D_A),
        "q_norm_g": gain(ks[8], Q_RANK),
        "w_uq": w(ks[9], (Q_RANK, N_HEADS * QK_DIM), Q_RANK),
        "kv_norm_g": gain(ks[10], KV_RANK),
        "w_ukv": w(ks[11], (KV_RANK, N_HEADS * (QK_NOPE + V_DIM)), KV_RANK),
        "q_head_g": gain(ks[12], QK_DIM),
        "k_head_g": gain(ks[13], QK_DIM),
        "w_b_o": w(ks[14], (N_HEADS * V_DIM, D_MODEL), N_HEADS * V_DIM),
        "w_out": w(ks[15], (D_MODEL, D_MODEL), D_MODEL),
        "norm2_g": gain(ks[16], D_MODEL),
        "w_up": w(ks[17], (D_MODEL, 2 * D_FF), D_MODEL),
        "conv_w": w(ks[18], (CONV_W, 2 * D_FF), CONV_W),
        "conv_b": 0.02 * jax.random.normal(ks[19], (2 * D_FF,), f32),
        "w_down": w(ks[20], (D_FF, D_MODEL), D_FF),
    }


def reference(x, positions, norm1_g, w_in, v_ln_g, v_ln_b, w_s, b_s, w_a_o,
              q_norm_g, w_uq, kv_norm_g, w_ukv, q_head_g, k_head_g, w_b_o, w_out,
              norm2_g, w_up, conv_w, conv_b, w_down):
    B, S, D = x.shape
    for _layer in range(DEPTH):
        z = rms_norm(x, norm1_g) @ w_in
        cuts = np.cumsum([D_A, D_A, Q_RANK, KV_RANK, QK_ROPE]).tolist()
        u, v, c_q, c_kv, k_rope, gate_logits = jnp.split(z, cuts, axis=-1)
        y_a = gmlp_branch(u, v, v_ln_g, v_ln_b, w_s, b_s, w_a_o)
        y_b = mla_branch(c_q, c_kv, k_rope, positions, q_norm_g, w_uq, kv_norm_g, w_ukv,
                         q_head_g, k_head_g, w_b_o)
        gates = jax.nn.sigmoid(gate_logits).reshape(B, S, 2, D)
        merged = gates[:, :, 0, :] * y_a + gates[:, :, 1, :] * y_b
        x = x + merged @ w_out
        x = x + conv_gated_mlp(x, norm2_g, w_up, conv_w, conv_b, w_down)
    return x
```

```python
import math
import numpy as np
import concourse.bass as bass
import concourse.mybir as mybir
from concourse.bass_utils import run_bass_kernel_spmd
from contextlib import ExitStack

F32 = mybir.dt.float32
BF16 = mybir.dt.bfloat16
I32 = mybir.dt.int32
U8 = mybir.dt.uint8
AF = mybir.ActivationFunctionType
ALU = mybir.AluOpType
AX = mybir.AxisListType

D = 1024
S = 8192
TQ = 2048
NQ = 2050
NXQ = 2304
H = 8
DFF = 2816
NPAIR = 22
EPS = 1e-6
QTILES = [(0, 512, 0), (512, 512, 512), (1024, 512, 1024), (1536, 512, 1536), (2048, 2, 2175)]
TWO_PI = 2.0 * math.pi
C1 = 6.28125
C2 = TWO_PI - C1
PI_SAFE = 3.1415925

ENGS = ("pe", "act", "dve", "pool", "sp")


class Res:
    __slots__ = ("name", "w", "rs", "rd", "excl")

    def __init__(self, name, excl=False):
        self.name = name
        self.excl = excl
        self.w = None
        self.rs = {}
        self.rd = []


class Op:
    __slots__ = ("eng", "fn", "idx", "deps", "sig", "dma", "needed", "phase")


class Sched:
    ND = 16

    def __init__(self, nc, es):
        self.nc = nc
        self.ops = {e: [] for e in ENGS}
        self.seen = {e: {} for e in ENGS}
        self.seen_dma = {e: set() for e in ENGS}
        self.phase = 0
        self.es = es
        self.sems = {}
        if SHARED_DMA_SEMS:
            pool_ = [es.enter_context(nc.semaphore("dma%d" % i)) for i in range(24)]
            self.dsems = {"sp": pool_, "pool": pool_}
            dl, dc, di_ = [None] * 24, [0] * 24, [0]
            self.dlast = {"sp": dl, "pool": dl}
            self.dcnt = {"sp": dc, "pool": dc}
            self.dic = {"sp": di_, "pool": di_}
            self.NDq = 24
        else:
            self.dsems = {q: [es.enter_context(nc.semaphore("dma_%s%d" % (q, i))) for i in range(self.ND)] for q in ("sp", "pool")}
            self.dlast = {q: [None] * self.ND for q in ("sp", "pool")}
            self.dcnt = {q: [0] * self.ND for q in ("sp", "pool")}
            self.dic = {"sp": [0], "pool": [0]}
            self.NDq = self.ND
        self.outs = []
        self.bar = {e: [] for e in ENGS}
        self.alldma = []

    def barrier(self):
        deps = [self.ops[e][-1] for e in ENGS if self.ops[e] and not self.ops[e][-1].dma]
        for e in ENGS:
            last = [o for o in self.ops[e] if not o.dma]
            if last:
                deps.append(last[-1])
        deps = list({id(d): d for d in deps}.values()) + list(self.alldma)
        self.alldma = []
        for e in ENGS:
            self.bar[e] = list(deps)

    def sem(self, phase, eng):
        k = (phase, eng)
        if k not in self.sems:
            self.sems[k] = self.es.enter_context(self.nc.semaphore("s_%s_%d" % (eng, phase)))
        return self.sems[k]

    def add(self, eng, fn, r=(), w=(), dma=False, out=False):
        op = Op()
        op.eng = eng
        op.fn = fn
        op.dma = dma
        op.needed = False
        op.phase = self.phase
        op.sig = None
        op.idx = len(self.ops[eng])
        xr = [res for res in r if res.excl]
        if xr:
            r = [res for res in r if not res.excl]
            w = list(w) + [res for res in xr if res not in w]
        deps = []
        for res in r:
            if res.w is not None:
                deps.append(res.w)
        for res in w:
            if res.w is not None:
                deps.append(res.w)
            deps.extend(res.rs.values())
            deps.extend(res.rd)
        if self.bar[eng]:
            deps.extend(self.bar[eng])
            self.bar[eng] = []
        if dma:
            self.alldma.append(op)
            slot = self.dic[eng][0] % self.NDq
            self.dic[eng][0] += 1
            if self.dlast[eng][slot] is not None:
                deps.append(self.dlast[eng][slot])
            self.dlast[eng][slot] = op
            self.dcnt[eng][slot] += 1
            op.sig = (self.dsems[eng][slot], 16 * self.dcnt[eng][slot])
            op.needed = True
        need = []
        for d in deps:
            if d is op:
                continue
            if d.dma:
                if d in self.seen_dma[eng]:
                    continue
                self.seen_dma[eng].add(d)
                need.append(d)
            else:
                if d.eng == eng and eng == "pe":
                    continue
                if self.seen[eng].get(d.eng, -1) >= d.idx:
                    continue
                self.seen[eng][d.eng] = d.idx
                d.needed = True
                need.append(d)
        op.deps = need
        for res in r:
            if dma:
                res.rd.append(op)
            else:
                res.rs[eng] = op
        for res in w:
            res.w = op
            res.rs = {}
            res.rd = []
        self.ops[eng].append(op)
        if out:
            self.outs.append(op)
        return op

    def emit(self, block):
        cnt = {}
        for eng in ENGS:
            for op in self.ops[eng]:
                if op.dma or not op.needed:
                    continue
                k = (op.phase, eng)
                cnt[k] = cnt.get(k, 0) + 1
                assert cnt[k] < 60000
                op.sig = (self.sem(op.phase, eng), cnt[k])
        deco = {"pe": block.tensor, "act": block.scalar, "dve": block.vector, "pool": block.gpsimd, "sp": block.sync}
        for eng in ENGS:
            ops = self.ops[eng]
            outs = self.outs if eng == "sp" else []

            def body(e, ops=ops, outs=outs):
                for op in ops:
                    for d in op.deps:
                        e.wait_ge(d.sig[0], d.sig[1])
                    inst = op.fn(e)
                    if op.sig is not None:
                        inst.then_inc(op.sig[0], 16 if op.dma else 1)
                for o in outs:
                    e.wait_ge(o.sig[0], o.sig[1])

            if ops or outs:
                deco[eng](body)


class Mem:
    def __init__(self, big, cap):
        self.big = big
        self.cap = cap
        self.free_list = [(0, cap)]
        self.live = {}
        self.tags = {}

    def alloc(self, n, dt, tag=None):
        sz = {F32: 4, BF16: 2, I32: 4}[dt]
        nb = (n * sz + ALIGN - 1) // ALIGN * ALIGN
        for i, (o, l) in enumerate(self.free_list):
            if l >= nb:
                if l == nb:
                    self.free_list.pop(i)
                else:
                    self.free_list[i] = (o + nb, l - nb)
                ap = self.big[:, o:o + n * sz].bitcast(dt)
                self.live[o] = nb
                if tag is not None:
                    self.tags.setdefault(tag, []).append(o)
                return ap
        raise AssertionError(("SBUF overflow", nb, self.free_list))

    def free(self, *tags):
        for t in tags:
            for o in self.tags.pop(t):
                nb = self.live.pop(o)
                self.free_list.append((o, nb))
        self.free_list.sort()
        m = []
        for o, l in self.free_list:
            if m and m[-1][0] + m[-1][1] == o:
                m[-1] = (m[-1][0], m[-1][1] + l)
            else:
                m.append((o, l))
        self.free_list = m


COLS = {}
_c = 0
for _n, _w in [("n1g", 8), ("qng", 3), ("kvng", 2), ("qhg_n", 1), ("khg_n", 1), ("qgB", 1), ("qgC", 1), ("kgB", 1), ("kgC", 1),
               ("n2g", 8), ("cw0", 44), ("cw1", 44), ("cw2", 44), ("cb", 44), ("invf", 1), ("sgn", 1), ("hmask", 2)]:
    COLS[_n] = (_c, _w)
    _c += _w
NCOLS = _c


ILW = 2
ALIGN = 256
RS_MODE = "hybrid"
RS_EVERY = 8
RS_POOL = True
PROD_POOL = True
SHARED_DMA_SEMS = False
SQ_ON_ACT = True


def build(stop_after=None, dbg=None):
    nc = bass.Bass("TRN2", target_bir_lowering=False)

    def din(name, shape, dt=F32):
        return nc.dram_tensor(name, list(shape), dt, kind="ExternalInput").ap()

    xkv = din("xkv", [D, S])
    xq = din("xq", [D, NXQ])
    posk = din("posk", [1, S], I32)
    posq = din("posq", [1, NQ], I32)
    cols_d = din("cols", [128, NCOLS])
    wA_d = din("wA", [D, 896])
    wuq_d = din("wuq", [384, H * 384])
    wkv_d = din("wkv", [256, 2048])
    wC_d = din("wC", [D, 4096])
    wao_d = din("wao", [D, D])
    wbo_d = din("wbo", [D, D])
    wout_d = din("wout", [D, D])
    wup_d = din("wup", [NPAIR, D, 256])
    wdn_d = din("wdn", [8, 128, NPAIR * 128])
    wsT_d = din("wsT", [128, 1024])
    bs_d = din("bs", [1, 1024])
    vlng_d = din("vlng", [1, 1024])
    vlnb_d = din("vlnb", [1, 1024])
    out_d = nc.dram_tensor("out", [D, TQ], F32, kind="ExternalOutput").ap()
    dbg_out = {}
    if dbg:
        for k, (shp, dt_) in dbg.items():
            dbg_out[k] = nc.dram_tensor("dbg_" + k, list(shp), dt_, kind="ExternalOutput").ap()

    es = ExitStack()
    CAP = 212736
    big = es.enter_context(nc.sbuf_tensor("big", [128, CAP], U8))
    psum = es.enter_context(nc.psum_tensor("psum", [128, 8 * 512], F32))
    PB = [psum[:, i * 512:(i + 1) * 512] for i in range(8)]
    PR = [Res("ps%d" % i, excl=True) for i in range(8)]
    sch = Sched(nc, es)
    mem = Mem(big, CAP)

    def pe(fn, r=(), w=()):
        return sch.add("pe", fn, r, w)

    def act(fn, r=(), w=()):
        return sch.add("act", fn, r, w)

    def dve(fn, r=(), w=()):
        return sch.add("dve", fn, r, w)

    def pool(fn, r=(), w=()):
        return sch.add("pool", fn, r, w)

    def dma(eng, out, in_, r=(), w=(), is_out=False, **kw):
        return sch.add(eng, lambda e: e.dma_start(out=out, in_=in_, **kw), r, w, dma=True, out=is_out)

    def mm(out, pairs, r, w):
        n = len(pairs)

        def fn(e):
            inst = None
            for i, (l, rh) in enumerate(pairs):
                inst = e.matmul(out, lhsT=l, rhs=rh, start=(i == 0), stop=(i == n - 1))
            return inst
        return pe(fn, r, w)

    def wload(dst3, src, k, res, extra_w=()):
        n = src.shape[1]
        step = 2048
        for c0 in range(0, n, step):
            c1 = min(n, c0 + step)
            for kk in range(k):
                dma("pool", dst3[:, kk, c0:c1], src[kk * 128:(kk + 1) * 128, c0:c1], w=[res] + list(extra_w), max_dma_last_dim=8192)

    def tap(name, ap, res):
        if dbg and name in dbg_out:
            dma("sp", dbg_out[name], ap, r=[res], is_out=True)

    cols = mem.alloc(NCOLS, F32)
    R_cols = Res("cols")
    dma("sp", cols, cols_d, w=[R_cols])

    def col(name, j=0, rows=128):
        o, w_ = COLS[name]
        return cols[0:rows, o + j:o + j + 1]

    ones = mem.alloc(128, BF16)
    R_ones = Res("ones")
    dve(lambda e: e.memset(ones, 1.0), w=[R_ones])
    gqk = mem.alloc(1, F32)
    R_gqk = Res("gqk")
    dve(lambda e: e.tensor_tensor(out=gqk, in0=col("qhg_n"), in1=col("khg_n"), op=ALU.mult), r=[R_cols], w=[R_gqk])

    wkv = mem.alloc(2 * 2048, BF16, "kv1").rearrange("p (k n) -> p k n", k=2)
    R_wkv = Res("wkv")
    wload(wkv, wkv_d, 2, R_wkv)

    kvnT = mem.alloc(2 * S, BF16, "kv1").rearrange("p (k n) -> p k n", k=2)
    R_kvn = [Res("kvn%d" % t) for t in range(16)]
    RT = mem.alloc(S, BF16, "kv2")
    R_RT = [Res("RT%d" % t) for t in range(16)]
    rstdk = mem.alloc(512, F32, "kv2")
    R_rstdk = Res("rstdk")
    R_Q = [[Res("Q%d_%d" % (h, j)) for j in range(5)] for h in range(H)]

    def rope_tables(pos_d, c0, n, cos_t, sin_t, scr, R_t, R_scr):
        pi_ = scr["pi"][:, 0:n]
        ang = scr["ang"][:, 0:n]
        ki = scr["ki"][:, 0:n]
        r1 = scr["r1"][:, 0:n]
        dma("sp", pi_, pos_d[0:1, c0:c0 + n].partition_broadcast(128), w=[R_scr])
        dve(lambda e: e.tensor_scalar(out=ang, in0=pi_, scalar1=col("invf"), scalar2=None, op0=ALU.mult), r=[R_cols, R_scr], w=[R_scr])
        dve(lambda e: e.tensor_scalar(out=ki, in0=ang, scalar1=1.0 / TWO_PI, scalar2=None, op0=ALU.mult), r=[R_scr], w=[R_scr])
        dve(lambda e: e.scalar_tensor_tensor(out=r1, in0=ki, scalar=-C1, in1=ang, op0=ALU.mult, op1=ALU.add), r=[R_scr], w=[R_scr])
        dve(lambda e: e.scalar_tensor_tensor(out=ang, in0=ki, scalar=-C2, in1=r1, op0=ALU.mult, op1=ALU.add), r=[R_scr], w=[R_scr])
        dve(lambda e: e.tensor_scalar(out=r1, in0=ang, scalar1=PI_SAFE, scalar2=-PI_SAFE, op0=ALU.min, op1=ALU.max), r=[R_scr], w=[R_scr])
        act(lambda e: e.activation(out=sin_t, in_=r1, func=AF.Sin, scale=col("sgn")), r=[R_scr, R_cols], w=[R_t])
        act(lambda e: e.activation(out=ang, in_=r1, func=AF.Sin, scale=0.5), r=[R_scr], w=[R_scr])
        act(lambda e: e.activation(out=ang, in_=ang, func=AF.Square), r=[R_scr], w=[R_scr])
        dve(lambda e: e.tensor_scalar(out=cos_t, in0=ang, scalar1=-2.0, scalar2=1.0, op0=ALU.mult, op1=ALU.add), r=[R_scr], w=[R_t])

    def rstd_from_ssq(ps_ap, n_feat, lnv, out_ap, r, w_ln, w_out):
        act(lambda e: e.activation(out=lnv, in_=ps_ap, func=AF.Ln, scale=1.0 / n_feat, bias=epsc), r=list(r) + [R_eps], w=[w_ln])
        act(lambda e: e.activation(out=out_ap, in_=lnv, func=AF.Exp, scale=-0.5), r=[w_ln], w=[w_out])

    epsc = mem.alloc(1, F32)
    R_eps = Res("eps")
    dve(lambda e: e.memset(epsc, EPS), w=[R_eps])

    def TT(eng, out, in0, in1, op, r, w):
        return sch.add(eng, lambda e: e.tensor_tensor(out=out, in0=in0, in1=in1, op=op), r, w)

    def STT(out, in0, scalar, in1, op0, op1, r, w, accum=None):
        if accum is None:
            return sch.add("dve", lambda e: e.scalar_tensor_tensor(out=out, in0=in0, scalar=scalar, in1=in1, op0=op0, op1=op1), r, w)
        return sch.add("dve", lambda e: e.scalar_tensor_tensor(out=out, in0=in0, scalar=scalar, in1=in1, op0=op0, op1=op1, accum_out=accum), r, w)

    def TS(eng, out, in0, s1, s2, op0, op1, r, w):
        if s2 is None:
            return sch.add(eng, lambda e: e.tensor_scalar(out=out, in0=in0, scalar1=s1, scalar2=None, op0=op0), r, w)
        return sch.add(eng, lambda e: e.tensor_scalar(out=out, in0=in0, scalar1=s1, scalar2=s2, op0=op0, op1=op1), r, w)

    def AC(out, in_, func, r, w, **kw):
        return sch.add("act", lambda e: e.activation(out=out, in_=in_, func=func, **kw), r, w)

    def interleave(gens, width=None):
        width = ILW if width is None else width
        active = []
        gens = list(gens)
        while gens or active:
            while gens and len(active) < width:
                active.append(gens.pop(0))
            for g in list(active):
                try:
                    next(g)
                except StopIteration:
                    active.remove(g)

    class Slot:
        pass

    def make_slots(tag, banks, with_k=False, n_xs=1):
        sl = []
        for i in range(2):
            o = Slot()
            o.i = i
            o.b = banks[i]
            o.xs = mem.alloc(8 * 512, F32, tag + "xs").rearrange("p (k n) -> p k n", k=8)
            o.R_xs = Res("xs%d" % i)
            o.sq = mem.alloc(8 * 512, BF16, tag + "xs").rearrange("p (k n) -> p k n", k=8)
            o.R_sq = Res("sq%d" % i)
            o.xn = mem.alloc(8 * 512, BF16, tag + "xs").rearrange("p (k n) -> p k n", k=8)
            o.R_xn = Res("xn%d" % i)
            o.lnv = mem.alloc(512, F32, tag)
            o.R_lnv = Res("lnv%d" % i)
            o.rstd = mem.alloc(512, F32, tag)
            o.R_rstd = Res("rstd%d" % i)
            o.sqc = mem.alloc(3 * 512, BF16, tag).rearrange("p (k n) -> p k n", k=3)
            o.R_sqc = Res("sqc%d" % i)
            o.u1 = mem.alloc(512, F32, tag)
            o.u2 = mem.alloc(512, F32, tag)
            o.R_u1 = Res("u1_%d" % i)
            o.R_u2 = Res("u2_%d" % i)
            if with_k:
                o.sqK = mem.alloc(1024, BF16, tag + "k")
                o.R_sqK = Res("sqK%d" % i)
            sl.append(o)
        return sl

    def gen_norm1(src3, c0, n, o, xn_out, R_xn_out):
        b0 = o.b[0]
        dma("sp", o.xs[:, :, 0:n], src3[:, :, c0:c0 + n], w=[o.R_xs])
        yield
        for k in range(8):
            if SQ_ON_ACT and k % 3 == 2:
                AC(o.sq[:, k, 0:n], o.xs[:, k, 0:n], AF.Square, r=[o.R_xs], w=[o.R_sq])
            else:
                TT("pool", o.sq[:, k, 0:n], o.xs[:, k, 0:n], o.xs[:, k, 0:n], ALU.mult, r=[o.R_xs], w=[o.R_sq])
        yield
        mm(PB[b0][:, 0:n], [(ones, o.sq[:, k, 0:n]) for k in range(8)], r=[R_ones, o.R_sq], w=[PR[b0]])
        yield
        AC(o.lnv[:, 0:n], PB[b0][:, 0:n], AF.Ln, r=[PR[b0], R_eps], w=[o.R_lnv], scale=1.0 / D, bias=epsc)
        AC(PB[b0][:, 0:n], o.lnv[:, 0:n], AF.Exp, r=[o.R_lnv], w=[PR[b0]], scale=-0.5)
        yield
        for k in range(8):
            STT(xn_out[:, k, 0:n], o.xs[:, k, 0:n], col("n1g", k), PB[b0][:, 0:n], ALU.mult, ALU.mult, r=[o.R_xs, PR[b0], R_cols], w=[R_xn_out])
        yield

    sch.phase = 1
    wA = mem.alloc(8 * 896, BF16, "wA").rearrange("p (k n) -> p k n", k=8)
    R_wA = Res("wA")
    wload(wA, wA_d, 8, R_wA)
    SL = make_slots("sl", [[0, 1, 2, 3], [4, 5, 6, 7]], with_k=True)
    ssqk = mem.alloc(512, F32, "a1")
    R_ssqk = Res("ssqk")
    HK = 1024
    cosk = [mem.alloc(HK, F32, "tab") for _ in range(2)]
    sink = [mem.alloc(HK, F32, "tab") for _ in range(2)]
    R_tk = [Res("tabk0"), Res("tabk1")]
    scr = {k: mem.alloc(HK, I32 if k in ("pi", "ki") else F32, "tab") for k in ("pi", "ang", "ki", "r1")}
    R_scr = Res("scr")
    xkv3 = xkv.rearrange("(k p) t -> p k t", p=128)
    xq3 = xq.rearrange("(k p) t -> p k t", p=128)

    def gen_a1(t):
        o = SL[t % 2]
        b0, b1, b2, b3 = o.b
        tb = (t // 2) % 2
        if t % 2 == 0:
            rope_tables(posk, t * 512, HK, cosk[tb], sink[tb], scr, R_tk[tb], R_scr)
            yield
        tc0 = (t % 2) * 512
        ksl = slice(t * 512, (t + 1) * 512)
        yield from gen_norm1(xkv3, t * 512, 512, o, o.xn, o.R_xn)
        for c in range(2):
            mm(PB[b1 + c], [(wA[:, k, 384 + c * 128:384 + (c + 1) * 128], o.xn[:, k, :]) for k in range(8)], r=[R_wA, o.R_xn], w=[PR[b1 + c]])
        mm(PB[b3], [(wA[:, k, 640:768], o.xn[:, k, :]) for k in range(8)], r=[R_wA, o.R_xn], w=[PR[b3]])
        yield
        for c in range(2):
            AC(o.sqc[:, c, :], PB[b1 + c], AF.Square, r=[PR[b1 + c]], w=[o.R_sqc])
        AC(o.sqc[:, 2, :], PB[b3], AF.Square, r=[PR[b3]], w=[o.R_sqc])
        yield
        mm(PB[b0], [(ones, o.sqc[:, 0, :]), (ones, o.sqc[:, 1, :])], r=[R_ones, o.R_sqc], w=[PR[b0]])
        yield
        AC(o.lnv, PB[b0], AF.Ln, r=[PR[b0], R_eps], w=[o.R_lnv], scale=1.0 / 256, bias=epsc)
        AC(o.rstd, o.lnv, AF.Exp, r=[o.R_lnv], w=[o.R_rstd], scale=-0.5)
        STT(o.u1, PB[b3], col("kgB"), cosk[tb][:, tc0:tc0 + 512], ALU.mult, ALU.mult, r=[PR[b3], R_tk[tb], R_cols], w=[o.R_u1])
        yield
        mm(PB[b3], [(wA[:, k, 768:896], o.xn[:, k, :]) for k in range(8)], r=[R_wA, o.R_xn], w=[PR[b3]])

        def n1(e):
            inst = None
            for j in range(4):
                inst = e.matmul(PB[b0][:, j:j + 1], lhsT=o.sqc[:, 2, j * 128:(j + 1) * 128], rhs=ones[:, 0:1], start=True, stop=True)
            return inst
        pe(n1, r=[o.R_sqc, R_ones], w=[PR[b0]])
        yield
        for c in range(2):
            STT(kvnT[:, c, ksl], PB[b1 + c], col("kvng", c), o.rstd, ALU.mult, ALU.mult, r=[PR[b1 + c], o.R_rstd, R_cols], w=[R_kvn[t]])
        STT(o.u2, PB[b3], col("kgC"), sink[tb][:, tc0:tc0 + 512], ALU.mult, ALU.mult, r=[PR[b3], R_tk[tb], R_cols], w=[o.R_u2])
        yield
        TT("pool", RT[:, ksl], o.u1, o.u2, ALU.add, r=[o.R_u1, o.R_u2], w=[R_RT[t]])
        for j in range(4):
            kt = t * 4 + j
            kk = slice(kt * 128, (kt + 1) * 128)

            def tk(e, kk=kk):
                inst = None
                for half in range(2):
                    for c in range(2):
                        inst = e.matmul(PB[b1 + half], lhsT=kvnT[:, c, kk], rhs=wkv[:, c, half * 512:(half + 1) * 512], start=(c == 0), stop=(c == 1))
                return inst
            pe(tk, r=[R_kvn[t], R_wkv], w=[PR[b1], PR[b2]])
            yield
            AC(o.sqK, psum[:, b1 * 512:(b1 + 2) * 512], AF.Square, r=[PR[b1], PR[b2]], w=[o.R_sqK])
            yield
            sch.add("dve", (lambda e, o_=ssqk[:, kt * 8:(kt + 1) * 8], i_=o.sqK.rearrange("p (a b) -> p a b", a=8): e.tensor_reduce(out=o_, in_=i_, axis=AX.X, op=ALU.add)),
                    r=[o.R_sqK], w=[R_ssqk])
            TS("dve", ssqk[:, kt * 8:(kt + 1) * 8], ssqk[:, kt * 8:(kt + 1) * 8], PB[b0][:, j:j + 1], None, ALU.add, None, r=[PR[b0], R_ssqk], w=[R_ssqk])
            yield

    interleave([gen_a1(t) for t in range(16)])
    lnsc = mem.alloc(1, F32, "a1")
    R_lnsc = Res("lnsc")
    dve(lambda e: e.memset(lnsc, -0.5 * math.log(192.0)), w=[R_lnsc])
    AC(ssqk, ssqk, AF.Ln, r=[R_ssqk, R_eps], w=[R_ssqk], scale=1.0 / 192.0, bias=epsc)
    AC(rstdk, ssqk, AF.Exp, r=[R_ssqk, R_lnsc], w=[R_rstdk], scale=-0.5, bias=lnsc)
    tap("kvnT0", kvnT[:, 0, :], R_kvn[15])
    tap("RT", RT, R_RT[15])
    tap("rstdk", rstdk, R_rstdk)

    if stop_after == "A1":
        dve(lambda e: e.memset(SL[0].lnv, 0.0), w=[SL[0].R_lnv])
        for k in range(8):
            dma("sp", out_d[k * 128:(k + 1) * 128, 0:512], SL[0].lnv, r=[SL[0].R_lnv], is_out=True)
        with nc.Block() as block:
            sch.emit(block)
        es.close()
        return nc

    sch.phase = 2
    sch.barrier()
    mem.free("a1", "tab", "slk")
    cqn = mem.alloc(3 * NQ, BF16, "cqn").rearrange("p (k n) -> p k n", k=3)
    R_cqn = [Res("cqn%d" % j) for j in range(5)]

    def gen_a2a(j):
        d0, n, s0 = QTILES[j]
        o = SL[j % 2]
        b0, b1, b2, b3 = o.b
        yield from gen_norm1(xq3, s0, n, o, o.xn, o.R_xn)
        for c in range(3):
            mm(PB[b1 + c][:, 0:n], [(wA[:, k, c * 128:(c + 1) * 128], o.xn[:, k, 0:n]) for k in range(8)], r=[R_wA, o.R_xn], w=[PR[b1 + c]])
        yield
        for c in range(3):
            AC(o.sqc[:, c, 0:n], PB[b1 + c][:, 0:n], AF.Square, r=[PR[b1 + c]], w=[o.R_sqc])
        yield
        mm(PB[b0][:, 0:n], [(ones, o.sqc[:, c, 0:n]) for c in range(3)], r=[R_ones, o.R_sqc], w=[PR[b0]])
        yield
        AC(o.lnv[:, 0:n], PB[b0][:, 0:n], AF.Ln, r=[PR[b0], R_eps], w=[o.R_lnv], scale=1.0 / 384, bias=epsc)
        AC(o.rstd[:, 0:n], o.lnv[:, 0:n], AF.Exp, r=[o.R_lnv], w=[o.R_rstd], scale=-0.5)
        yield
        for c in range(3):
            STT(cqn[:, c, d0:d0 + n], PB[b1 + c][:, 0:n], col("qng", c), o.rstd[:, 0:n], ALU.mult, ALU.mult, r=[PR[b1 + c], o.R_rstd, R_cols], w=[R_cqn[j]])
        yield

    interleave([gen_a2a(j) for j in range(5)])
    sch.barrier()
    mem.free("slxs", "wA")
    QN = mem.alloc(H * NQ, BF16, "Q").rearrange("p (h n) -> p h n", h=H)
    QR = mem.alloc(H * NQ, BF16, "Q").rearrange("p (h n) -> p h n", h=H)
    wuq = mem.alloc(3 * H * 384, BF16, "wuq").rearrange("p (k n) -> p k n", k=3)
    R_wuq = Res("wuq")
    wload(wuq, wuq_d, 3, R_wuq)
    cosq = mem.alloc(NQ, F32, "tabq")
    sinq = mem.alloc(NQ, F32, "tabq")
    R_tq = Res("tabq")
    scrq = {k: mem.alloc(512, I32 if k in ("pi", "ki") else F32, "tabq") for k in ("pi", "ang", "ki", "r1")}
    for (d0_, n_, _s) in QTILES:
        rope_tables(posq, d0_, n_, cosq[:, d0_:d0_ + n_], sinq[:, d0_:d0_ + n_], scrq, R_tq, R_scr)

    def gen_a2b(j, h, o):
        d0, n, s0 = QTILES[j]
        cs = slice(d0, d0 + n)
        b0, b1, b2, b3 = o.b
        for i3 in range(3):
            mm(PB[b1 + i3][:, 0:n], [(wuq[:, c, h * 384 + i3 * 128:h * 384 + (i3 + 1) * 128], cqn[:, c, cs]) for c in range(3)],
               r=[R_wuq, R_cqn[j]], w=[PR[b1 + i3]])
        yield
        for i3 in range(2):
            AC(o.sqc[:, i3, 0:n], PB[b1 + i3][:, 0:n], AF.Square, r=[PR[b1 + i3]], w=[o.R_sqc])
        yield
        mm(PB[b0][:, 0:n], [(ones, o.sqc[:, 0, 0:n]), (ones, o.sqc[:, 1, 0:n])], r=[R_ones, o.R_sqc], w=[PR[b0]])
        STT(o.u1[:, 0:n], PB[b2][:, 0:n], col("qgB"), cosq[:, cs], ALU.mult, ALU.mult, r=[PR[b2], R_tq, R_cols], w=[o.R_u1])
        STT(o.u2[:, 0:n], PB[b3][:, 0:n], col("qgC"), sinq[:, cs], ALU.mult, ALU.mult, r=[PR[b3], R_tq, R_cols], w=[o.R_u2])
        yield
        AC(o.lnv[:, 0:n], PB[b0][:, 0:n], AF.Ln, r=[PR[b0], R_eps], w=[o.R_lnv], scale=1.0 / 192, bias=epsc)
        AC(o.rstd[:, 0:n], o.lnv[:, 0:n], AF.Exp, r=[o.R_lnv], w=[o.R_rstd], scale=-0.5)
        TT("pool", o.u1[:, 0:n], o.u1[:, 0:n], o.u2[:, 0:n], ALU.add, r=[o.R_u1, o.R_u2], w=[o.R_u1])
        yield
        STT(QN[:, h, cs], PB[b1][:, 0:n], gqk, o.rstd[:, 0:n], ALU.mult, ALU.mult, r=[PR[b1], o.R_rstd, R_gqk], w=[R_Q[h][j]])
        TT("dve", QR[:, h, cs], o.u1[:, 0:n], o.rstd[:, 0:n], ALU.mult, r=[o.R_u1, o.R_rstd], w=[R_Q[h][j]])
        yield

    gl = []
    ii = 0
    for j in range(5):
        for h in range(H):
            gl.append(gen_a2b(j, h, SL[ii % 2]))
            ii += 1
    interleave(gl)
    tap("QN0", QN[:, 0, :], R_Q[0][4])
    tap("QR0", QR[:, 0, :], R_Q[0][4])
    tap("QN7", QN[:, 7, :], R_Q[7][4])

    if stop_after == "A":
        dve(lambda e: e.memset(SL[0].lnv, 0.0), w=[SL[0].R_lnv])
        for k in range(8):
            dma("sp", out_d[k * 128:(k + 1) * 128, 0:512], SL[0].lnv, r=[SL[0].R_lnv], is_out=True)
        with nc.Block() as block:
            sch.emit(block)
        es.close()
        return nc

    sch.phase = 3
    sch.barrier()
    mem.free("wuq", "cqn", "tabq", "sl")
    OT = mem.alloc(H * NQ, BF16, "OT").rearrange("p (h n) -> p h n", h=H)
    R_OT = [Res("OT%d" % h) for h in range(H)]
    KT = mem.alloc(S, BF16, "B")
    VV = mem.alloc(S, BF16, "B")
    R_KT = [Res("KT%d" % g) for g in range(16)]
    R_VV = [Res("VV%d" % g) for g in range(16)]
    NP = 4
    Pb = [mem.alloc(512, BF16, "B") for _ in range(NP)]
    R_P = [Res("P%d" % i) for i in range(NP)]
    rinv = mem.alloc(512, F32, "B")
    accS = mem.alloc(512, F32, "B")
    R_accS = Res("accS")
    accP = mem.alloc(512, F32, "B")
    R_accP = Res("accP")
    onesf = mem.alloc(128, F32, "B")
    R_onesf = Res("onesf")
    dve(lambda e: e.memset(onesf, 1.0), w=[R_onesf])
    R_rinv = Res("rinv")
    SB = [0, 1, 2]

    def gen_kv(h, g):
        gs = slice(g * 512, (g + 1) * 512)
        mm(PB[6], [(wkv[:, c, h * 128:(h + 1) * 128], kvnT[:, c, gs]) for c in range(2)], r=[R_wkv, R_kvn[g]], w=[PR[6]])
        dve(lambda e: e.tensor_copy(out=KT[:, gs], in_=PB[6]), r=[PR[6]], w=[R_KT[g]])

        def vg(e):
            inst = None
            for j in range(4):
                kk = slice(g * 512 + j * 128, g * 512 + (j + 1) * 128)
                for c in range(2):
                    inst = e.matmul(PB[7][:, j * 128:(j + 1) * 128], lhsT=kvnT[:, c, kk], rhs=wkv[:, c, 1024 + h * 128:1024 + (h + 1) * 128],
                                    start=(c == 0), stop=(c == 1))
            return inst
        pe(vg, r=[R_wkv, R_kvn[g]], w=[PR[7]])
        dve(lambda e: e.tensor_copy(out=VV[:, gs], in_=PB[7]), r=[PR[7]], w=[R_VV[g]])

    for g in range(16):
        gen_kv(0, g)
    BT = [(i * 410, 410) for i in range(5)]
    heads = range(H) if stop_after != "B1" else range(1)
    wC_pref = False
    for h in heads:
        if h == H - 1 and stop_after is None:
            mem.free("kv1")
            wCb = mem.alloc(8 * 2052, BF16, "wC")[:, 0:8 * 2048].rearrange("p (k n) -> p k n", k=8)
            R_wC = Res("wC")
            wload(wCb, wC_d[:, 0:2048], 8, R_wC, extra_w=R_kvn + [R_wkv])
            wC_pref = True
        for qi, (d0, n) in enumerate(BT):
            cs = slice(d0, d0 + n)
            ob = 3 if RS_MODE == "hybrid" else 3 + (qi % 2)
            last_q = (qi == len(BT) - 1)

            def qk(kt):
                sb = SB[kt % 3]
                ks = slice(kt * 128, (kt + 1) * 128)
                mm(PB[sb][:, 0:n], [(KT[:, ks], QN[:, h, cs]), (RT[:, ks], QR[:, h, cs])],
                   r=[R_KT[kt // 4], R_RT[kt // 4]] + R_Q[h], w=[PR[sb]])

            qk(0)
            qk(1)
            for kt in range(64):
                sb = SB[kt % 3]
                pb = kt % NP
                if kt + 2 < 64:
                    qk(kt + 2)
                act(lambda e, sb=sb, pb=pb, kt=kt, n=n, h=h: e.activation(out=Pb[pb][:, 0:n], in_=PB[sb][:, 0:n], func=AF.Exp, scale=rstdk[:, kt * 8 + h:kt * 8 + h + 1]),
                    r=[PR[sb], R_rstdk], w=[R_P[pb]])
                ks = slice(kt * 128, (kt + 1) * 128)
                pe(lambda e, ks=ks, pb=pb, kt=kt, ob=ob, n=n: e.matmul(PB[ob][:, 0:n], lhsT=VV[:, ks], rhs=Pb[pb][:, 0:n], start=(kt == 0), stop=(kt == 63)),
                   r=[R_VV[kt // 4], R_P[pb]], w=[PR[ob]])
                if RS_MODE == "hybrid":
                    if kt % RS_EVERY == 0 and RS_POOL:
                        if kt == 0:
                            sch.add("pool", (lambda e, o_=accP[:, 0:n], i_=Pb[pb][:, 0:n]: e.tensor_copy(out=o_, in_=i_)), r=[R_P[pb]], w=[R_accP])
                        else:
                            TT("pool", accP[:, 0:n], Pb[pb][:, 0:n], accP[:, 0:n], ALU.add, r=[R_P[pb], R_accP], w=[R_accP])
                    elif kt % RS_EVERY == 0:
                        pe(lambda e, pb=pb, kt=kt, n=n: e.matmul(PB[5][:, 0:n], lhsT=ones, rhs=Pb[pb][:, 0:n], start=(kt == 0), stop=False),
                           r=[R_ones, R_P[pb]], w=[PR[5]])
                    elif kt == 1:
                        sch.add("dve", (lambda e, o_=PB[4][:, 0:n], i_=Pb[pb][:, 0:n]: e.tensor_copy(out=o_, in_=i_)), r=[R_P[pb]], w=[PR[4]])
                    else:
                        TT("dve", PB[4][:, 0:n], Pb[pb][:, 0:n], PB[4][:, 0:n], ALU.add, r=[R_P[pb], PR[4]], w=[PR[4]])
                else:
                    pe(lambda e, pb=pb, kt=kt, n=n: e.matmul(PB[5][:, 0:n], lhsT=ones, rhs=Pb[pb][:, 0:n], start=(kt == 0), stop=(kt == 63)),
                       r=[R_ones, R_P[pb]], w=[PR[5]])
                if last_q and h + 1 < H and kt % 4 == 3 and stop_after != "B1":
                    gen_kv(h + 1, kt // 4)
            if RS_MODE == "hybrid":
                sch.add("dve", (lambda e, o_=accS[:, 0:n], i_=PB[4][:, 0:n]: e.tensor_copy(out=o_, in_=i_)), r=[PR[4]], w=[R_accS])
                if RS_POOL:
                    sch.add("pe", (lambda e, o_=PB[5][:, 0:n], r_=accP[:, 0:n]: e.matmul(o_, lhsT=onesf, rhs=r_, start=True, stop=False)), r=[R_onesf, R_accP], w=[PR[5]])
                sch.add("pe", (lambda e, o_=PB[5][:, 0:n], r_=accS[:, 0:n]: e.matmul(o_, lhsT=onesf, rhs=r_, start=False, stop=True)), r=[R_onesf, R_accS], w=[PR[5]])
            dve(lambda e, n=n: e.reciprocal(out=rinv[:, 0:n], in_=PB[5][:, 0:n]), r=[PR[5]], w=[R_rinv])
            dve(lambda e, ob=ob, h=h, cs=cs, n=n: e.tensor_tensor(out=OT[:, h, cs], in0=PB[ob][:, 0:n], in1=rinv[:, 0:n], op=ALU.mult), r=[PR[ob], R_rinv], w=[R_OT[h]])
    tap("OT0", OT[:, 0, :], R_OT[0])
    if stop_after in ("B", "B1"):
        dve(lambda e: e.memset(rinv, 0.0), r=[R_rinv], w=[R_rinv])
        for k in range(8):
            dma("sp", out_d[k * 128:(k + 1) * 128, 0:512], rinv, r=[R_rinv], is_out=True)
        with nc.Block() as block:
            sch.emit(block)
        es.close()
        return nc

    sch.phase = 4
    sch.barrier()
    mem.free("Q", "B", "kv2")
    if not wC_pref:
        mem.free("kv1")
    XT = [(0, 512), (512, 512), (1024, 512), (1536, 512), (2048, 256)]
    xnq = mem.alloc(8 * NXQ, BF16, "xnq").rearrange("p (k n) -> p k n", k=8)
    R_xnq = [Res("xnq%d" % i) for i in range(5)]
    if not wC_pref:
        wCb = mem.alloc(8 * 2052, BF16, "wC")[:, 0:8 * 2048].rearrange("p (k n) -> p k n", k=8)
        R_wC = Res("wC")
        wload(wCb, wC_d[:, 0:2048], 8, R_wC)
    SLc = make_slots("c", [[0, 1, 2, 3], [4, 5, 6, 7]])

    def gen_xnq(i):
        c0, n = XT[i]
        yield from gen_norm1(xq3, c0, n, SLc[i % 2], xnq[:, :, c0:c0 + n], R_xnq[i])
    interleave([gen_xnq(i) for i in range(5)])
    sch.barrier()
    mem.free("cxs", "c")

    def xsrc(j):
        d0, n, s0 = QTILES[j]
        return xnq[:, :, s0:s0 + n], R_xnq[4 if j == 4 else j]

    wsT = mem.alloc(1024, BF16, "c2")
    bsr = mem.alloc(1024, BF16, "c2")
    vg_b = mem.alloc(1024, F32, "c2")
    vb_b = mem.alloc(1024, F32, "c2")
    R_c2 = Res("c2consts")
    dma("pool", wsT, wsT_d, w=[R_c2], max_dma_last_dim=4096)
    dma("pool", bsr[0:1, :], bs_d, w=[R_c2], max_dma_last_dim=4096)
    dma("sp", vg_b, vlng_d.partition_broadcast(128), w=[R_c2])
    dma("sp", vb_b, vlnb_d.partition_broadcast(128), w=[R_c2])
    mhalf = mem.alloc(1, F32, "c2")
    R_mh = Res("mhalf")
    dve(lambda e: e.memset(mhalf, -0.5), w=[R_mh])
    yT = mem.alloc(8 * NQ, BF16, "yT").rearrange("p (k n) -> p k n", k=8)
    R_yT = [Res("yT%d" % j) for j in range(5)]
    uT = [mem.alloc(8 * 512, F32, "c2").rearrange("p (k n) -> p k n", k=8) for _ in range(2)]
    R_uT = [Res("uT0"), Res("uT1")]

    class VS:
        pass
    VSL = []
    NVS = 4
    vsq_sh = mem.alloc(1024, BF16, "c2")
    R_vsq_sh = Res("vsq")
    for i in range(NVS):
        o = VS()
        o.vgl = mem.alloc(1024, F32, "c2")
        o.R_vgl = Res("vgl%d" % i)
        o.vsq = vsq_sh
        o.R_vsq = R_vsq_sh
        o.vln = mem.alloc(1024, BF16, "c2")
        o.R_vln = Res("vln%d" % i)
        o.st = mem.alloc(8, F32, "c2")
        o.R_st = Res("st%d" % i)
        o.vb = (2, 3) if i % 2 == 0 else (4, 5)
        o.sb = (6, 7) if i % 2 == 0 else (0, 1)
        VSL.append(o)

    def u_block(j, ub):
        d0, n, s0 = QTILES[j]
        src, rs = xsrc(j)
        for f in range(8):
            pb = f % 2
            mm(PB[pb][:, 0:n], [(wCb[:, k, f * 128:(f + 1) * 128], src[:, k, :]) for k in range(8)], r=[R_wC, rs], w=[PR[pb]])
            AC(uT[ub][:, f, 0:n], PB[pb][:, 0:n], AF.Gelu_apprx_tanh, r=[PR[pb]], w=[R_uT[ub]])

    def gen_v(c0, rs, o, fin):
        v0, v1 = o.vb
        st = o.st

        def vm(e):
            inst = None
            for half in range(2):
                for k in range(8):
                    inst = e.matmul(PB[v0 + half], lhsT=xnq[:, k, c0:c0 + 128], rhs=wCb[:, k, 1024 + half * 512:1024 + (half + 1) * 512], start=(k == 0), stop=(k == 7))
            return inst
        pe(vm, r=[R_wC, rs], w=[PR[v0], PR[v1]])
        AC(o.vgl, psum[:, v0 * 512:(v0 + 2) * 512], AF.Gelu_apprx_tanh, r=[PR[v0], PR[v1]], w=[o.R_vgl, o.R_st], accum_out=st[:, 0:1])
        yield
        STT(o.vsq, o.vgl, 1.0, o.vgl, ALU.mult, ALU.mult, r=[o.R_vgl, o.R_st], w=[o.R_vsq, o.R_st], accum=st[:, 1:2])
        yield
        TS("dve", st[:, 2:3], st[:, 0:1], 1.0 / 1024, None, ALU.mult, None, r=[o.R_st], w=[o.R_st])
        TT("dve", st[:, 6:7], st[:, 2:3], st[:, 2:3], ALU.mult, r=[o.R_st], w=[o.R_st])
        STT(st[:, 3:4], st[:, 1:2], 1.0 / 1024, st[:, 6:7], ALU.mult, ALU.subtract, r=[o.R_st], w=[o.R_st])
        TS("dve", st[:, 3:4], st[:, 3:4], EPS, None, ALU.add, None, r=[o.R_st], w=[o.R_st])
        yield
        TT("pool", st[:, 4:5], st[:, 3:4], mhalf, ALU.pow, r=[o.R_st, R_mh], w=[o.R_st])
        yield
        STT(st[:, 5:6], st[:, 2:3], -1.0, st[:, 4:5], ALU.mult, ALU.mult, r=[o.R_st], w=[o.R_st])
        TS("dve", o.vgl, o.vgl, st[:, 4:5], st[:, 5:6], ALU.mult, ALU.add, r=[o.R_st, o.R_vgl], w=[o.R_vgl])
        yield
        TT("pool", o.vgl, o.vgl, vg_b, ALU.mult, r=[o.R_vgl, R_c2], w=[o.R_vgl])
        TT("pool", o.vln, o.vgl, vb_b, ALU.add, r=[o.R_vgl, R_c2], w=[o.R_vln])
        yield
        if fin is None:
            return
        do_v2(o, fin)
        yield

    def do_v2(o, fin):
        s0_, s1_ = o.sb

        def sp(e):
            inst = None
            for g in range(8):
                oo = PB[(s0_, s1_)[g // 4]][:, (g % 4) * 128:(g % 4 + 1) * 128]
                e.matmul(oo, lhsT=o.vln[:, g * 128:(g + 1) * 128], rhs=wsT[:, g * 128:(g + 1) * 128], start=True, stop=False)
                inst = e.matmul(oo, lhsT=ones[0:1, :], rhs=bsr[0:1, g * 128:(g + 1) * 128], start=False, stop=True)
            return inst
        pe(sp, r=[o.R_vln, R_c2, R_ones], w=[PR[s0_], PR[s1_]])
        fin(o)

    def fin_main(j, c, ub):
        c0 = j * 512 + c * 128

        def f(o):
            for hh in range(2):
                TT("dve", yT[:, hh * 4:(hh + 1) * 4, c0:c0 + 128], PB[o.sb[hh]].rearrange("p (a b) -> p a b", a=4),
                   uT[ub][:, hh * 4:(hh + 1) * 4, c * 128:(c + 1) * 128], ALU.mult, r=[PR[o.sb[hh]], R_uT[ub]], w=[R_yT[j]])
        return f

    def fin_halo(side):
        pc = 127 if side == 0 else 0

        def f(o):
            for hh in range(2):
                TT("dve", yT[:, hh * 4:(hh + 1) * 4, 2048 + side:2049 + side], PB[o.sb[hh]].rearrange("p (a b) -> p a b", a=4)[:, :, pc:pc + 1],
                   uT[0][:, hh * 4:(hh + 1) * 4, side:side + 1], ALU.mult, r=[PR[o.sb[hh]], R_uT[0]], w=[R_yT[4]])
        return f

    u_block(0, 0)
    for j in range(4):
        ub = j % 2
        interleave([gen_v(j * 512 + c * 128, R_xnq[j], VSL[c], None) for c in range(4)], width=4)
        if j + 1 < 4:
            u_block(j + 1, (j + 1) % 2)
        else:
            u_block(4, 0)
        for c in range(4):
            do_v2(VSL[c], fin_main(j, c, ub))
    interleave([gen_v(2048 + side * 128, R_xnq[4], VSL[side], None) for side in range(2)], width=2)
    for side in range(2):
        do_v2(VSL[side], fin_halo(side))
    tap("yT0", yT[:, 0, :], R_yT[4])
    sch.barrier()
    mem.free("c2")
    mem.free("wC")
    wout = mem.alloc(8 * D, BF16, "wout").rearrange("p (k n) -> p k n", k=8)
    R_wout = Res("wout")
    mg = mem.alloc(8 * NQ, BF16, "mg").rearrange("p (k n) -> p k n", k=8)
    R_mg = [Res("mg%d" % j) for j in range(5)]
    NW3 = 2
    w3b = [[mem.alloc(8 * 128, BF16, "w3").rearrange("p (k n) -> p k n", k=8) for _ in range(4)] for _ in range(NW3)]
    R_w3 = [Res("w3_%d" % i) for i in range(NW3)]
    wao3 = wao_d.rearrange("(k p) n -> p k n", p=128)
    wbo3 = wbo_d.rearrange("(k p) n -> p k n", p=128)
    wC3 = wC_d.rearrange("(k p) n -> p k n", p=128)

    def load_w3(f):
        if f < 8:
            bb = f % NW3
            fs = slice(f * 128, (f + 1) * 128)
            dma("pool", w3b[bb][0], wao3[:, :, fs], w=[R_w3[bb]], max_dma_last_dim=4096)
            dma("pool", w3b[bb][1], wC3[:, :, 2048 + f * 128:2048 + (f + 1) * 128], w=[R_w3[bb]], max_dma_last_dim=4096)
            dma("pool", w3b[bb][2], wbo3[:, :, fs], w=[R_w3[bb]], max_dma_last_dim=4096)
            dma("pool", w3b[bb][3], wC3[:, :, 3072 + f * 128:3072 + (f + 1) * 128], w=[R_w3[bb]], max_dma_last_dim=4096)

    class MS:
        pass
    MSL = []
    for i in range(2):
        o = MS()
        o.b = [0, 1, 2, 3] if i == 0 else [4, 5, 6, 7]
        o.sA = mem.alloc(512, F32, "c3")
        o.sB = mem.alloc(512, F32, "c3")
        o.m1 = mem.alloc(512, F32, "c3")
        o.m2 = mem.alloc(512, F32, "c3")
        o.R_sA, o.R_sB, o.R_m1, o.R_m2 = Res("sA%d" % i), Res("sB%d" % i), Res("m1%d" % i), Res("m2%d" % i)
        MSL.append(o)

    def gen_mg(f, j, o):
        d0, n, s0 = QTILES[j]
        src, rs = xsrc(j)
        cs = slice(d0, d0 + n)
        bb = f % NW3
        w = w3b[bb]
        b0, b1, b2, b3 = o.b
        mm(PB[b0][:, 0:n], [(w[0][:, k, :], yT[:, k, cs]) for k in range(8)], r=[R_w3[bb], R_yT[j]], w=[PR[b0]])
        mm(PB[b1][:, 0:n], [(w[1][:, k, :], src[:, k, :]) for k in range(8)], r=[R_w3[bb], rs], w=[PR[b1]])
        mm(PB[b2][:, 0:n], [(w[2][:, k, :], OT[:, k, cs]) for k in range(8)], r=[R_w3[bb]] + R_OT, w=[PR[b2]])
        mm(PB[b3][:, 0:n], [(w[3][:, k, :], src[:, k, :]) for k in range(8)], r=[R_w3[bb], rs], w=[PR[b3]])
        yield
        AC(o.sA[:, 0:n], PB[b1][:, 0:n], AF.Sigmoid, r=[PR[b1]], w=[o.R_sA])
        AC(o.sB[:, 0:n], PB[b3][:, 0:n], AF.Sigmoid, r=[PR[b3]], w=[o.R_sB])
        yield
        TT("dve", o.m1[:, 0:n], PB[b0][:, 0:n], o.sA[:, 0:n], ALU.mult, r=[PR[b0], o.R_sA], w=[o.R_m1])
        TT("dve", o.m2[:, 0:n], PB[b2][:, 0:n], o.sB[:, 0:n], ALU.mult, r=[PR[b2], o.R_sB], w=[o.R_m2])
        yield
        TT("pool", mg[:, f, cs], o.m1[:, 0:n], o.m2[:, 0:n], ALU.add, r=[o.R_m1, o.R_m2], w=[R_mg[j]])
        yield

    load_w3(0)
    wload(wout, wout_d, 8, R_wout)
    gi = 0
    for f in range(8):
        load_w3(f + 1)
        gl = []
        for j in range(5):
            gl.append(gen_mg(f, j, MSL[gi % 2]))
            gi += 1
        interleave(gl)
    sch.barrier()
    mem.free("w3", "c3", "OT", "yT", "xnq")
    sch.phase = 5
    hT = mem.alloc(8 * NQ, F32, "hT").rearrange("p (k n) -> p k n", k=8)
    R_hT = [Res("hT%d" % j) for j in range(5)]
    hn = mem.alloc(8 * NQ, BF16, "hn").rearrange("p (k n) -> p k n", k=8)
    R_hn = Res("hn")

    class HS:
        pass
    HSL = []
    for i in range(2):
        o = HS()
        o.b = (4, 5, 0) if i == 0 else (6, 7, 1)
        o.xs = mem.alloc(8 * 512, F32, "xs").rearrange("p (k n) -> p k n", k=8)
        o.R_xs = Res("xs3_%d" % i)
        o.sq = mem.alloc(8 * 512, BF16, "d0").rearrange("p (k n) -> p k n", k=8)
        o.R_sq = Res("sqd%d" % i)
        o.lnv = mem.alloc(512, F32, "d0")
        o.R_lnv = Res("lnvd%d" % i)
        HSL.append(o)

    def gen_h(j):
        d0, n, s0 = QTILES[j]
        o = HSL[j % 2]
        cs = slice(d0, d0 + n)
        dma("sp", o.xs[:, :, 0:n], xq3[:, :, s0:s0 + n], w=[o.R_xs])
        yield
        for f in range(8):
            pb = o.b[f % 2]
            mm(PB[pb][:, 0:n], [(wout[:, k, f * 128:(f + 1) * 128], mg[:, k, cs]) for k in range(8)], r=[R_wout, R_mg[j]], w=[PR[pb]])
            TT("dve", hT[:, f, cs], PB[pb][:, 0:n], o.xs[:, f, 0:n], ALU.add, r=[PR[pb], o.R_xs], w=[R_hT[j]])
            if f % 2 == 1:
                yield
        for k in range(8):
            if k % 2 == 0:
                TT("pool", o.sq[:, k, 0:n], hT[:, k, cs], hT[:, k, cs], ALU.mult, r=[R_hT[j]], w=[o.R_sq])
            else:
                AC(o.sq[:, k, 0:n], hT[:, k, cs], AF.Square, r=[R_hT[j]], w=[o.R_sq])
        yield
        b0 = o.b[2]
        mm(PB[b0][:, 0:n], [(ones, o.sq[:, k, 0:n]) for k in range(8)], r=[R_ones, o.R_sq], w=[PR[b0]])
        yield
        AC(o.lnv[:, 0:n], PB[b0][:, 0:n], AF.Ln, r=[PR[b0], R_eps], w=[o.R_lnv], scale=1.0 / D, bias=epsc)
        AC(PB[b0][:, 0:n], o.lnv[:, 0:n], AF.Exp, r=[o.R_lnv], w=[PR[b0]], scale=-0.5)
        yield
        for k in range(8):
            STT(hn[:, k, cs], hT[:, k, cs], col("n2g", k), PB[b0][:, 0:n], ALU.mult, ALU.mult, r=[R_hT[j], PR[b0], R_cols], w=[R_hn])
        yield

    interleave([gen_h(j) for j in range(5)])
    tap("hT0", hT[:, 0, :], R_hT[4])
    cwm = mem.alloc(2 * 44, F32, "dm")
    R_cwm = Res("cwm")
    o0, _ = COLS["cw0"]
    o2, _ = COLS["cw2"]
    dve(lambda e: e.tensor_scalar(out=cwm[:, 0:44], in0=cols[:, o0:o0 + 44], scalar1=col("hmask", 0), scalar2=None, op0=ALU.mult), r=[R_cols], w=[R_cwm])
    dve(lambda e: e.tensor_scalar(out=cwm[:, 44:88], in0=cols[:, o2:o2 + 44], scalar1=col("hmask", 1), scalar2=None, op0=ALU.mult), r=[R_cols], w=[R_cwm])
    sch.barrier()
    mem.free("d0", "wout", "mg", "xs")
    actT = mem.alloc(NPAIR * 1024, BF16, "actT").rearrange("p (k n) -> p k n", k=NPAIR)
    R_actT = Res("actT")
    NWB = 4
    wupb = [mem.alloc(8 * 256, BF16, "wup").rearrange("p (k n) -> p k n", k=8) for _ in range(NWB)]
    R_wup = [Res("wup%d" % i) for i in range(NWB)]
    NDB = 4
    wdnb = [mem.alloc(NPAIR * 128, BF16, "wdn") for _ in range(NDB)]
    R_wdn = [Res("wdn%d" % i) for i in range(NDB)]
    acc4 = [mem.alloc(1024, F32, "acc") for _ in range(4)]
    R_acc4 = [Res("acc%d" % i) for i in range(4)]
    obuf = [mem.alloc(512, F32, "acc") for _ in range(2)]
    R_ob = [Res("ob0"), Res("ob1")]
    steps = [(hf, j) for hf in range(2) for j in range(NPAIR)]

    def load_wup(si):
        if si < len(steps):
            _, j = steps[si]
            dma("pool", wupb[si % NWB], wup_d[j].rearrange("(k p) n -> p k n", p=128), w=[R_wup[si % NWB]], max_dma_last_dim=4096)

    def load_wdn(di):
        if di < 16:
            dma("pool", wdnb[di % NDB], wdn_d[di % 8], w=[R_wdn[di % NDB]], max_dma_last_dim=8192)

    load_wup(0)
    load_wup(1)
    oi = 0
    for si, (hf, j) in enumerate(steps):
        load_wup(si + 2)
        if j == NPAIR - 4:
            load_wdn(hf * 8)
            load_wdn(hf * 8 + 1)
        wb = si % NWB
        t0 = hf * 1024
        acc = acc4[2 * (si % 2):2 * (si % 2) + 2]
        R_acc = R_acc4[2 * (si % 2):2 * (si % 2) + 2]
        if hf == 0:
            hal = hn[:, :, 1024:2049:1024]
            li, ri = 1, 0
        else:
            hal = hn[:, :, 1023:2050:1026]
            li, ri = 0, 1
        for vg in range(2):
            cc = j + 22 * vg
            base = 2 * vg * 512
            up = psum[:, base:base + 1024]
            for t2 in range(2):
                mm(PB[2 * vg + t2], [(wupb[wb][:, k, vg * 128:(vg + 1) * 128], hn[:, k, t0 + t2 * 512:t0 + (t2 + 1) * 512]) for k in range(8)],
                   r=[R_wup[wb], R_hn], w=[PR[2 * vg + t2]])
            hp = PB[4 + vg]
            mm(hp[:, 0:2], [(wupb[wb][:, k, vg * 128:(vg + 1) * 128], hal[:, k, :]) for k in range(8)], r=[R_wup[wb], R_hn], w=[PR[4 + vg]])
            a = acc[vg]
            prs = [PR[2 * vg], PR[2 * vg + 1]]
            sch.add("act", (lambda e, o=a, i=up, sc=col("cw1", cc), bi=col("cb", cc): e.activation(out=o, in_=i, func=AF.Identity, scale=sc, bias=bi)),
                    r=prs + [R_cols], w=[R_acc[vg]])
            sch.add("dve", (lambda e, o=a[:, 1:1024], i0=up[:, 0:1023], sc=col("cw0", cc), i1=a[:, 1:1024]: e.scalar_tensor_tensor(out=o, in0=i0, scalar=sc, in1=i1, op0=ALU.mult, op1=ALU.add)),
                    r=prs + [R_cols, R_acc[vg]], w=[R_acc[vg]])
            sch.add("dve", (lambda e, o=a[:, 0:1023], i0=up[:, 1:1024], sc=col("cw2", cc), i1=a[:, 0:1023]: e.scalar_tensor_tensor(out=o, in0=i0, scalar=sc, in1=i1, op0=ALU.mult, op1=ALU.add)),
                    r=prs + [R_cols, R_acc[vg]], w=[R_acc[vg]])
            lsc = cwm[:, cc:cc + 1] if hf == 0 else col("cw0", cc)
            rsc = col("cw2", cc) if hf == 0 else cwm[:, 44 + cc:45 + cc]
            sch.add("dve", (lambda e, o=a[:, 0:1], i0=hp[:, li:li + 1], sc=lsc, i1=a[:, 0:1]: e.scalar_tensor_tensor(out=o, in0=i0, scalar=sc, in1=i1, op0=ALU.mult, op1=ALU.add)),
                    r=[PR[4 + vg], R_cols, R_cwm, R_acc[vg]], w=[R_acc[vg]])
            sch.add("dve", (lambda e, o=a[:, 1023:1024], i0=hp[:, ri:ri + 1], sc=rsc, i1=a[:, 1023:1024]: e.scalar_tensor_tensor(out=o, in0=i0, scalar=sc, in1=i1, op0=ALU.mult, op1=ALU.add)),
                    r=[PR[4 + vg], R_cols, R_cwm, R_acc[vg]], w=[R_acc[vg]])
        sch.add("act", (lambda e, o=acc[1]: e.activation(out=o, in_=o, func=AF.Silu)), r=[R_acc[1]], w=[R_acc[1]])
        sch.add("pool" if PROD_POOL else "dve", (lambda e, o=actT[:, j, :], i0=acc[0], i1=acc[1]: e.tensor_tensor(out=o, in0=i0, in1=i1, op=ALU.mult)), r=[R_acc[0], R_acc[1]], w=[R_actT])
        if j == NPAIR - 1:
            for f in range(8):
                di = hf * 8 + f
                if f + 2 < 8:
                    load_wdn(di + 2)
                db = di % NDB
                for t2 in range(2):
                    pb = 6 + t2
                    ob = oi % 2
                    oi += 1
                    mm(PB[pb], [(wdnb[db][:, jj * 128:(jj + 1) * 128], actT[:, jj, t2 * 512:(t2 + 1) * 512]) for jj in range(NPAIR)], r=[R_wdn[db], R_actT], w=[PR[pb]])
                    c0 = t0 + t2 * 512
                    sch.add("dve", (lambda e, o=obuf[ob], i0=PB[pb], i1=hT[:, f, c0:c0 + 512]: e.tensor_tensor(out=o, in0=i0, in1=i1, op=ALU.add)),
                            r=[PR[pb]] + R_hT, w=[R_ob[ob]])
                    dma("sp", out_d[f * 128:(f + 1) * 128, c0:c0 + 512], obuf[ob], r=[R_ob[ob]], is_out=True)
    with nc.Block() as block:
        sch.emit(block)
    es.close()
    return nc


def _bf(x):
    return np.ascontiguousarray(x, dtype=np.float32)


def prep_inputs(inputs):
    x = np.asarray(inputs["x"], np.float32)
    pos = np.asarray(inputs["positions"], np.int32)
    g = {k: np.asarray(v) for k, v in inputs.items()}
    w_in = g["w_in"]
    u_c, v_c = w_in[:, 0:1024], w_in[:, 1024:2048]
    cq_c, ckv_c, kr_c = w_in[:, 2048:2432], w_in[:, 2432:2688], w_in[:, 2688:2752]
    gA_c, gB_c = w_in[:, 2752:3776], w_in[:, 3776:4800]
    z64 = np.zeros((D, 64), np.float32)
    kr_perm = np.concatenate([kr_c[:, 32:64], kr_c[:, 0:32]], axis=1)
    wA = np.concatenate([cq_c, ckv_c, kr_c, z64, kr_perm, z64], axis=1)
    wuq = g["w_uq"].reshape(384, H, 192)
    z = np.zeros((384, 64), np.float32)
    parts = []
    for h in range(H):
        r = wuq[:, h, 128:192]
        parts += [wuq[:, h, 0:128], r, z, np.concatenate([r[:, 32:64], r[:, 0:32]], axis=1), z]
    wuq_l = np.concatenate(parts, axis=1)
    wukv = g["w_ukv"].reshape(256, H, 256)
    wkv = np.concatenate([wukv[:, :, 0:128].reshape(256, 1024), wukv[:, :, 128:256].reshape(256, 1024)], axis=1)
    wC = np.concatenate([u_c, v_c, gA_c, gB_c], axis=1)
    wup = g["w_up"]
    wup_l = np.stack([np.concatenate([wup[:, j * 128:(j + 1) * 128], wup[:, DFF + j * 128:DFF + (j + 1) * 128]], axis=1) for j in range(NPAIR)])
    wsT = np.transpose(g["w_s"], (2, 0, 1)).reshape(128, 1024)
    cols = np.zeros((128, NCOLS), np.float32)

    def put(name, arr):
        o, w_ = COLS[name]
        cols[:arr.shape[0], o:o + w_] = arr
    put("n1g", g["norm1_g"].reshape(8, 128).T)
    put("qng", g["q_norm_g"].reshape(3, 128).T)
    put("kvng", g["kv_norm_g"].reshape(2, 128).T)
    qh, kh = g["q_head_g"], g["k_head_g"]
    put("qhg_n", qh[0:128, None])
    put("khg_n", kh[0:128, None])
    put("qgB", qh[128:192, None])
    put("qgC", np.concatenate([qh[160:192], qh[128:160]])[:, None])
    put("kgB", kh[128:192, None])
    put("kgC", np.concatenate([kh[160:192], kh[128:160]])[:, None])
    put("n2g", g["norm2_g"].reshape(8, 128).T)
    cw, cb = g["conv_w"], g["conv_b"]
    put("cw0", cw[0].reshape(44, 128).T)
    put("cw1", cw[1].reshape(44, 128).T)
    put("cw2", cw[2].reshape(44, 128).T)
    put("cb", cb.reshape(44, 128).T)
    invf = (np.float32(10000.0) ** (-np.arange(0, 64, 2, dtype=np.float32) / np.float32(64))).astype(np.float32)
    put("invf", np.tile(invf, 4)[:, None])
    sgn = np.ones(128, np.float32)
    sgn[0:32] = -1.0
    put("sgn", sgn[:, None])
    shared = dict(wA=wA, wuq=wuq_l, wkv=wkv, wC=wC, wao=g["w_a_o"], wbo=g["w_b_o"], wout=g["w_out"], wup=wup_l, wdn=np.ascontiguousarray(g["w_down"].reshape(NPAIR, 128, 8, 128).transpose(2, 1, 0, 3)).reshape(8, 128, NPAIR * 128),
                  wsT=wsT, bs=g["b_s"].reshape(1, 1024), vlng=g["v_ln_g"].reshape(1, 1024), vlnb=g["v_ln_b"].reshape(1, 1024))
    shared = {k: _bf(v) for k, v in shared.items()}
    xT = [np.ascontiguousarray(x[b].T) for b in range(2)]
    in_maps = []
    for c in range(8):
        b, qi = c // 4, c % 4
        t0 = qi * TQ
        xq = np.zeros((D, NXQ), np.float32)
        xq[:, 0:TQ] = xT[b][:, t0:t0 + TQ]
        pq = np.zeros((1, NQ), np.int32)
        pq[0, 0:TQ] = pos[b, t0:t0 + TQ]
        cc = cols.copy()
        o, _ = COLS["hmask"]
        if qi > 0:
            xq[:, TQ:TQ + 128] = xT[b][:, t0 - 128:t0]
            pq[0, TQ] = pos[b, t0 - 1]
            cc[:, o] = 1.0
        if qi < 3:
            xq[:, TQ + 128:TQ + 256] = xT[b][:, t0 + TQ:t0 + TQ + 128]
            pq[0, TQ + 1] = pos[b, t0 + TQ]
            cc[:, o + 1] = 1.0
        m = dict(shared)
        m.update(xkv=xT[b], xq=xq, posk=np.ascontiguousarray(pos[b:b + 1]), posq=pq, cols=cc)
        in_maps.append(m)
    return in_maps


_NC = None


def kernel(**inputs):
    global _NC
    in_maps = prep_inputs(inputs)
    if _NC is None:
        _NC = build()
    res = run_bass_kernel_spmd(_NC, in_maps, core_ids=list(range(8)))
    out = np.zeros((2, S, D), np.float32)
    for c in range(8):
        b, qi = c // 4, c % 4
        out[b, qi * TQ:(qi + 1) * TQ, :] = res.results[c]["out"].T
    return out
```

```python
import math
import numpy as np
import concourse.bass as bass
import concourse.mybir as mybir
from concourse.bass_utils import run_bass_kernel_spmd
from contextlib import ExitStack

F32 = mybir.dt.float32
BF16 = mybir.dt.bfloat16
I32 = mybir.dt.int32
U8 = mybir.dt.uint8
AF = mybir.ActivationFunctionType
ALU = mybir.AluOpType
AX = mybir.AxisListType

D = 1024
S = 8192
TQ = 2048
NQ = 2050
NXQ = 2304
H = 8
DFF = 2816
NPAIR = 22
EPS = 1e-6
QTILES = [(0, 512, 0), (512, 512, 512), (1024, 512, 1024), (1536, 512, 1536), (2048, 2, 2175)]
TWO_PI = 2.0 * math.pi
C1 = 6.28125
C2 = TWO_PI - C1
PI_SAFE = 3.1415925

ENGS = ("pe", "act", "dve", "pool", "sp")


class Res:
    __slots__ = ("name", "w", "rs", "rd", "excl")

    def __init__(self, name, excl=False):
        self.name = name
        self.excl = excl
        self.w = None
        self.rs = {}
        self.rd = []


class Op:
    __slots__ = ("eng", "fn", "idx", "deps", "sig", "dma", "needed", "phase")


class Sched:
    ND = 16

    def __init__(self, nc, es):
        self.nc = nc
        self.ops = {e: [] for e in ENGS}
        self.seen = {e: {} for e in ENGS}
        self.seen_dma = {e: set() for e in ENGS}
        self.phase = 0
        self.es = es
        self.sems = {}
        if SHARED_DMA_SEMS:
            pool_ = [es.enter_context(nc.semaphore("dma%d" % i)) for i in range(24)]
            self.dsems = {"sp": pool_, "pool": pool_}
            dl, dc, di_ = [None] * 24, [0] * 24, [0]
            self.dlast = {"sp": dl, "pool": dl}
            self.dcnt = {"sp": dc, "pool": dc}
            self.dic = {"sp": di_, "pool": di_}
            self.NDq = 24
        else:
            self.dsems = {q: [es.enter_context(nc.semaphore("dma_%s%d" % (q, i))) for i in range(self.ND)] for q in ("sp", "pool")}
            self.dlast = {q: [None] * self.ND for q in ("sp", "pool")}
            self.dcnt = {q: [0] * self.ND for q in ("sp", "pool")}
            self.dic = {"sp": [0], "pool": [0]}
            self.NDq = self.ND
        self.outs = []
        self.bar = {e: [] for e in ENGS}
        self.alldma = []

    def barrier(self):
        deps = [self.ops[e][-1] for e in ENGS if self.ops[e] and not self.ops[e][-1].dma]
        for e in ENGS:
            last = [o for o in self.ops[e] if not o.dma]
            if last:
                deps.append(last[-1])
        deps = list({id(d): d for d in deps}.values()) + list(self.alldma)
        self.alldma = []
        for e in ENGS:
            self.bar[e] = list(deps)

    def sem(self, phase, eng):
        k = (phase, eng)
        if k not in self.sems:
            self.sems[k] = self.es.enter_context(self.nc.semaphore("s_%s_%d" % (eng, phase)))
        return self.sems[k]

    def add(self, eng, fn, r=(), w=(), dma=False, out=False):
        op = Op()
        op.eng = eng
        op.fn = fn
        op.dma = dma
        op.needed = False
        op.phase = self.phase
        op.sig = None
        op.idx = len(self.ops[eng])
        xr = [res for res in r if res.excl]
        if xr:
            r = [res for res in r if not res.excl]
            w = list(w) + [res for res in xr if res not in w]
        deps = []
        for res in r:
            if res.w is not None:
                deps.append(res.w)
        for res in w:
            if res.w is not None:
                deps.append(res.w)
            deps.extend(res.rs.values())
            deps.extend(res.rd)
        if self.bar[eng]:
            deps.extend(self.bar[eng])
            self.bar[eng] = []
        if dma:
            self.alldma.append(op)
            slot = self.dic[eng][0] % self.NDq
            self.dic[eng][0] += 1
            if self.dlast[eng][slot] is not None:
                deps.append(self.dlast[eng][slot])
            self.dlast[eng][slot] = op
            self.dcnt[eng][slot] += 1
            op.sig = (self.dsems[eng][slot], 16 * self.dcnt[eng][slot])
            op.needed = True
        need = []
        for d in deps:
            if d is op:
                continue
            if d.dma:
                if d in self.seen_dma[eng]:
                    continue
                self.seen_dma[eng].add(d)
                need.append(d)
            else:
                if d.eng == eng and eng == "pe":
                    continue
                if self.seen[eng].get(d.eng, -1) >= d.idx:
                    continue
                self.seen[eng][d.eng] = d.idx
                d.needed = True
                need.append(d)
        op.deps = need
        for res in r:
            if dma:
                res.rd.append(op)
            else:
                res.rs[eng] = op
        for res in w:
            res.w = op
            res.rs = {}
            res.rd = []
        self.ops[eng].append(op)
        if out:
            self.outs.append(op)
        return op

    def emit(self, block):
        cnt = {}
        for eng in ENGS:
            for op in self.ops[eng]:
                if op.dma or not op.needed:
                    continue
                k = (op.phase, eng)
                cnt[k] = cnt.get(k, 0) + 1
                assert cnt[k] < 60000
                op.sig = (self.sem(op.phase, eng), cnt[k])
        deco = {"pe": block.tensor, "act": block.scalar, "dve": block.vector, "pool": block.gpsimd, "sp": block.sync}
        for eng in ENGS:
            ops = self.ops[eng]
            outs = self.outs if eng == "sp" else []

            def body(e, ops=ops, outs=outs):
                for op in ops:
                    for d in op.deps:
                        e.wait_ge(d.sig[0], d.sig[1])
                    inst = op.fn(e)
                    if op.sig is not None:
                        inst.then_inc(op.sig[0], 16 if op.dma else 1)
                for o in outs:
                    e.wait_ge(o.sig[0], o.sig[1])

            if ops or outs:
                deco[eng](body)


class Mem:
    def __init__(self, big, cap):
        self.big = big
        self.cap = cap
        self.free_list = [(0, cap)]
        self.live = {}
        self.tags = {}

    def alloc(self, n, dt, tag=None):
        sz = {F32: 4, BF16: 2, I32: 4}[dt]
        nb = (n * sz + ALIGN - 1) // ALIGN * ALIGN
        for i, (o, l) in enumerate(self.free_list):
            if l >= nb:
                if l == nb:
                    self.free_list.pop(i)
                else:
                    self.free_list[i] = (o + nb, l - nb)
                ap = self.big[:, o:o + n * sz].bitcast(dt)
                self.live[o] = nb
                if tag is not None:
                    self.tags.setdefault(tag, []).append(o)
                return ap
        raise AssertionError(("SBUF overflow", nb, self.free_list))

    def free(self, *tags):
        for t in tags:
            for o in self.tags.pop(t):
                nb = self.live.pop(o)
                self.free_list.append((o, nb))
        self.free_list.sort()
        m = []
        for o, l in self.free_list:
            if m and m[-1][0] + m[-1][1] == o:
                m[-1] = (m[-1][0], m[-1][1] + l)
            else:
                m.append((o, l))
        self.free_list = m


COLS = {}
_c = 0
for _n, _w in [("n1g", 8), ("qng", 3), ("kvng", 2), ("qhg_n", 1), ("khg_n", 1), ("qgB", 1), ("qgC", 1), ("kgB", 1), ("kgC", 1),
               ("n2g", 8), ("cw0", 44), ("cw1", 44), ("cw2", 44), ("cb", 44), ("invf", 1), ("sgn", 1), ("hmask", 2)]:
    COLS[_n] = (_c, _w)
    _c += _w
NCOLS = _c


ILW = 2
ALIGN = 256
RS_MODE = "hybrid"
RS_EVERY = 3
RS_POOL = True
PROD_POOL = True
SHARED_DMA_SEMS = False
SQ_ON_ACT = True


def build(stop_after=None, dbg=None):
    nc = bass.Bass("TRN2", target_bir_lowering=False)

    def din(name, shape, dt=F32):
        return nc.dram_tensor(name, list(shape), dt, kind="ExternalInput").ap()

    xkv = din("xkv", [D, S])
    xq = din("xq", [D, NXQ])
    posk = din("posk", [1, S], I32)
    posq = din("posq", [1, NQ], I32)
    cols_d = din("cols", [128, NCOLS])
    wA_d = din("wA", [D, 896])
    wuq_d = din("wuq", [384, H * 384])
    wkv_d = din("wkv", [256, 2048])
    wC_d = din("wC", [D, 4096])
    wao_d = din("wao", [D, D])
    wbo_d = din("wbo", [D, D])
    wout_d = din("wout", [D, D])
    wup_d = din("wup", [NPAIR, D, 256])
    wdn_d = din("wdn", [8, 128, NPAIR * 128])
    wsT_d = din("wsT", [128, 1024])
    bs_d = din("bs", [1, 1024])
    vlng_d = din("vlng", [1, 1024])
    vlnb_d = din("vlnb", [1, 1024])
    out_d = nc.dram_tensor("out", [D, TQ], F32, kind="ExternalOutput").ap()
    dbg_out = {}
    if dbg:
        for k, (shp, dt_) in dbg.items():
            dbg_out[k] = nc.dram_tensor("dbg_" + k, list(shp), dt_, kind="ExternalOutput").ap()

    es = ExitStack()
    CAP = 212736
    big = es.enter_context(nc.sbuf_tensor("big", [128, CAP], U8))
    psum = es.enter_context(nc.psum_tensor("psum", [128, 8 * 512], F32))
    PB = [psum[:, i * 512:(i + 1) * 512] for i in range(8)]
    PR = [Res("ps%d" % i, excl=True) for i in range(8)]
    sch = Sched(nc, es)
    mem = Mem(big, CAP)

    def pe(fn, r=(), w=()):
        return sch.add("pe", fn, r, w)

    def act(fn, r=(), w=()):
        return sch.add("act", fn, r, w)

    def dve(fn, r=(), w=()):
        return sch.add("dve", fn, r, w)

    def pool(fn, r=(), w=()):
        return sch.add("pool", fn, r, w)

    def dma(eng, out, in_, r=(), w=(), is_out=False, **kw):
        return sch.add(eng, lambda e: e.dma_start(out=out, in_=in_, **kw), r, w, dma=True, out=is_out)

    def mm(out, pairs, r, w):
        n = len(pairs)

        def fn(e):
            inst = None
            for i, (l, rh) in enumerate(pairs):
                inst = e.matmul(out, lhsT=l, rhs=rh, start=(i == 0), stop=(i == n - 1))
            return inst
        return pe(fn, r, w)

    def wload(dst3, src, k, res, extra_w=()):
        n = src.shape[1]
        step = 2048
        for c0 in range(0, n, step):
            c1 = min(n, c0 + step)
            for kk in range(k):
                dma("pool", dst3[:, kk, c0:c1], src[kk * 128:(kk + 1) * 128, c0:c1], w=[res] + list(extra_w), max_dma_last_dim=8192)

    def tap(name, ap, res):
        if dbg and name in dbg_out:
            dma("sp", dbg_out[name], ap, r=[res], is_out=True)

    cols = mem.alloc(NCOLS, F32)
    R_cols = Res("cols")
    dma("sp", cols, cols_d, w=[R_cols])

    def col(name, j=0, rows=128):
        o, w_ = COLS[name]
        return cols[0:rows, o + j:o + j + 1]

    ones = mem.alloc(128, BF16)
    R_ones = Res("ones")
    dve(lambda e: e.memset(ones, 1.0), w=[R_ones])
    gqk = mem.alloc(1, F32)
    R_gqk = Res("gqk")
    dve(lambda e: e.tensor_tensor(out=gqk, in0=col("qhg_n"), in1=col("khg_n"), op=ALU.mult), r=[R_cols], w=[R_gqk])

    wkv = mem.alloc(2 * 2048, BF16, "kv1").rearrange("p (k n) -> p k n", k=2)
    R_wkv = Res("wkv")
    wload(wkv, wkv_d, 2, R_wkv)

    kvnT = mem.alloc(2 * S, BF16, "kv1").rearrange("p (k n) -> p k n", k=2)
    R_kvn = [Res("kvn%d" % t) for t in range(16)]
    RT = mem.alloc(S, BF16, "kv2")
    R_RT = [Res("RT%d" % t) for t in range(16)]
    rstdk = mem.alloc(512, F32, "kv2")
    R_rstdk = Res("rstdk")
    R_Q = [[Res("Q%d_%d" % (h, j)) for j in range(5)] for h in range(H)]

    def rope_tables(pos_d, c0, n, cos_t, sin_t, scr, R_t, R_scr):
        pi_ = scr["pi"][:, 0:n]
        ang = scr["ang"][:, 0:n]
        ki = scr["ki"][:, 0:n]
        r1 = scr["r1"][:, 0:n]
        dma("sp", pi_, pos_d[0:1, c0:c0 + n].partition_broadcast(128), w=[R_scr])
        dve(lambda e: e.tensor_scalar(out=ang, in0=pi_, scalar1=col("invf"), scalar2=None, op0=ALU.mult), r=[R_cols, R_scr], w=[R_scr])
        dve(lambda e: e.tensor_scalar(out=ki, in0=ang, scalar1=1.0 / TWO_PI, scalar2=None, op0=ALU.mult), r=[R_scr], w=[R_scr])
        dve(lambda e: e.scalar_tensor_tensor(out=r1, in0=ki, scalar=-C1, in1=ang, op0=ALU.mult, op1=ALU.add), r=[R_scr], w=[R_scr])
        dve(lambda e: e.scalar_tensor_tensor(out=ang, in0=ki, scalar=-C2, in1=r1, op0=ALU.mult, op1=ALU.add), r=[R_scr], w=[R_scr])
        dve(lambda e: e.tensor_scalar(out=r1, in0=ang, scalar1=PI_SAFE, scalar2=-PI_SAFE, op0=ALU.min, op1=ALU.max), r=[R_scr], w=[R_scr])
        act(lambda e: e.activation(out=sin_t, in_=r1, func=AF.Sin, scale=col("sgn")), r=[R_scr, R_cols], w=[R_t])
        act(lambda e: e.activation(out=ang, in_=r1, func=AF.Sin, scale=0.5), r=[R_scr], w=[R_scr])
        act(lambda e: e.activation(out=ang, in_=ang, func=AF.Square), r=[R_scr], w=[R_scr])
        dve(lambda e: e.tensor_scalar(out=cos_t, in0=ang, scalar1=-2.0, scalar2=1.0, op0=ALU.mult, op1=ALU.add), r=[R_scr], w=[R_t])

    def rstd_from_ssq(ps_ap, n_feat, lnv, out_ap, r, w_ln, w_out):
        act(lambda e: e.activation(out=lnv, in_=ps_ap, func=AF.Ln, scale=1.0 / n_feat, bias=epsc), r=list(r) + [R_eps], w=[w_ln])
        act(lambda e: e.activation(out=out_ap, in_=lnv, func=AF.Exp, scale=-0.5), r=[w_ln], w=[w_out])

    epsc = mem.alloc(1, F32)
    R_eps = Res("eps")
    dve(lambda e: e.memset(epsc, EPS), w=[R_eps])

    def TT(eng, out, in0, in1, op, r, w):
        return sch.add(eng, lambda e: e.tensor_tensor(out=out, in0=in0, in1=in1, op=op), r, w)

    def STT(out, in0, scalar, in1, op0, op1, r, w, accum=None):
        if accum is None:
            return sch.add("dve", lambda e: e.scalar_tensor_tensor(out=out, in0=in0, scalar=scalar, in1=in1, op0=op0, op1=op1), r, w)
        return sch.add("dve", lambda e: e.scalar_tensor_tensor(out=out, in0=in0, scalar=scalar, in1=in1, op0=op0, op1=op1, accum_out=accum), r, w)

    def TS(eng, out, in0, s1, s2, op0, op1, r, w):
        if s2 is None:
            return sch.add(eng, lambda e: e.tensor_scalar(out=out, in0=in0, scalar1=s1, scalar2=None, op0=op0), r, w)
        return sch.add(eng, lambda e: e.tensor_scalar(out=out, in0=in0, scalar1=s1, scalar2=s2, op0=op0, op1=op1), r, w)

    def AC(out, in_, func, r, w, **kw):
        return sch.add("act", lambda e: e.activation(out=out, in_=in_, func=func, **kw), r, w)

    def interleave(gens, width=None):
        width = ILW if width is None else width
        active = []
        gens = list(gens)
        while gens or active:
            while gens and len(active) < width:
                active.append(gens.pop(0))
            for g in list(active):
                try:
                    next(g)
                except StopIteration:
                    active.remove(g)

    class Slot:
        pass

    def make_slots(tag, banks, with_k=False, n_xs=1):
        sl = []
        for i in range(2):
            o = Slot()
            o.i = i
            o.b = banks[i]
            o.xs = mem.alloc(8 * 512, F32, tag + "xs").rearrange("p (k n) -> p k n", k=8)
            o.R_xs = Res("xs%d" % i)
            o.sq = mem.alloc(8 * 512, BF16, tag + "xs").rearrange("p (k n) -> p k n", k=8)
            o.R_sq = Res("sq%d" % i)
            o.xn = mem.alloc(8 * 512, BF16, tag + "xs").rearrange("p (k n) -> p k n", k=8)
            o.R_xn = Res("xn%d" % i)
            o.lnv = mem.alloc(512, F32, tag)
            o.R_lnv = Res("lnv%d" % i)
            o.rstd = mem.alloc(512, F32, tag)
            o.R_rstd = Res("rstd%d" % i)
            o.sqc = mem.alloc(3 * 512, BF16, tag).rearrange("p (k n) -> p k n", k=3)
            o.R_sqc = Res("sqc%d" % i)
            o.u1 = mem.alloc(512, F32, tag)
            o.u2 = mem.alloc(512, F32, tag)
            o.R_u1 = Res("u1_%d" % i)
            o.R_u2 = Res("u2_%d" % i)
            if with_k:
                o.sqK = mem.alloc(1024, BF16, tag + "k")
                o.R_sqK = Res("sqK%d" % i)
            sl.append(o)
        return sl

    def gen_norm1(src3, c0, n, o, xn_out, R_xn_out):
        b0 = o.b[0]
        dma("sp", o.xs[:, :, 0:n], src3[:, :, c0:c0 + n], w=[o.R_xs])
        yield
        for k in range(8):
            if SQ_ON_ACT and k % 3 == 2:
                AC(o.sq[:, k, 0:n], o.xs[:, k, 0:n], AF.Square, r=[o.R_xs], w=[o.R_sq])
            else:
                TT("pool", o.sq[:, k, 0:n], o.xs[:, k, 0:n], o.xs[:, k, 0:n], ALU.mult, r=[o.R_xs], w=[o.R_sq])
        yield
        mm(PB[b0][:, 0:n], [(ones, o.sq[:, k, 0:n]) for k in range(8)], r=[R_ones, o.R_sq], w=[PR[b0]])
        yield
        AC(o.lnv[:, 0:n], PB[b0][:, 0:n], AF.Ln, r=[PR[b0], R_eps], w=[o.R_lnv], scale=1.0 / D, bias=epsc)
        AC(PB[b0][:, 0:n], o.lnv[:, 0:n], AF.Exp, r=[o.R_lnv], w=[PR[b0]], scale=-0.5)
        yield
        for k in range(8):
            STT(xn_out[:, k, 0:n], o.xs[:, k, 0:n], col("n1g", k), PB[b0][:, 0:n], ALU.mult, ALU.mult, r=[o.R_xs, PR[b0], R_cols], w=[R_xn_out])
        yield

    sch.phase = 1
    wA = mem.alloc(8 * 896, BF16, "wA").rearrange("p (k n) -> p k n", k=8)
    R_wA = Res("wA")
    wload(wA, wA_d, 8, R_wA)
    SL = make_slots("sl", [[0, 1, 2, 3], [4, 5, 6, 7]], with_k=True)
    ssqk = mem.alloc(512, F32, "a1")
    R_ssqk = Res("ssqk")
    HK = 1024
    cosk = [mem.alloc(HK, F32, "tab") for _ in range(2)]
    sink = [mem.alloc(HK, F32, "tab") for _ in range(2)]
    R_tk = [Res("tabk0"), Res("tabk1")]
    scr = {k: mem.alloc(HK, I32 if k in ("pi", "ki") else F32, "tab") for k in ("pi", "ang", "ki", "r1")}
    R_scr = Res("scr")
    xkv3 = xkv.rearrange("(k p) t -> p k t", p=128)
    xq3 = xq.rearrange("(k p) t -> p k t", p=128)

    def gen_a1(t):
        o = SL[t % 2]
        b0, b1, b2, b3 = o.b
        tb = (t // 2) % 2
        if t % 2 == 0:
            rope_tables(posk, t * 512, HK, cosk[tb], sink[tb], scr, R_tk[tb], R_scr)
            yield
        tc0 = (t % 2) * 512
        ksl = slice(t * 512, (t + 1) * 512)
        yield from gen_norm1(xkv3, t * 512, 512, o, o.xn, o.R_xn)
        for c in range(2):
            mm(PB[b1 + c], [(wA[:, k, 384 + c * 128:384 + (c + 1) * 128], o.xn[:, k, :]) for k in range(8)], r=[R_wA, o.R_xn], w=[PR[b1 + c]])
        mm(PB[b3], [(wA[:, k, 640:768], o.xn[:, k, :]) for k in range(8)], r=[R_wA, o.R_xn], w=[PR[b3]])
        yield
        for c in range(2):
            AC(o.sqc[:, c, :], PB[b1 + c], AF.Square, r=[PR[b1 + c]], w=[o.R_sqc])
        AC(o.sqc[:, 2, :], PB[b3], AF.Square, r=[PR[b3]], w=[o.R_sqc])
        yield
        mm(PB[b0], [(ones, o.sqc[:, 0, :]), (ones, o.sqc[:, 1, :])], r=[R_ones, o.R_sqc], w=[PR[b0]])
        yield
        AC(o.lnv, PB[b0], AF.Ln, r=[PR[b0], R_eps], w=[o.R_lnv], scale=1.0 / 256, bias=epsc)
        AC(o.rstd, o.lnv, AF.Exp, r=[o.R_lnv], w=[o.R_rstd], scale=-0.5)
        STT(o.u1, PB[b3], col("kgB"), cosk[tb][:, tc0:tc0 + 512], ALU.mult, ALU.mult, r=[PR[b3], R_tk[tb], R_cols], w=[o.R_u1])
        yield
        mm(PB[b3], [(wA[:, k, 768:896], o.xn[:, k, :]) for k in range(8)], r=[R_wA, o.R_xn], w=[PR[b3]])

        def n1(e):
            inst = None
            for j in range(4):
                inst = e.matmul(PB[b0][:, j:j + 1], lhsT=o.sqc[:, 2, j * 128:(j + 1) * 128], rhs=ones[:, 0:1], start=True, stop=True)
            return inst
        pe(n1, r=[o.R_sqc, R_ones], w=[PR[b0]])
        yield
        for c in range(2):
            STT(kvnT[:, c, ksl], PB[b1 + c], col("kvng", c), o.rstd, ALU.mult, ALU.mult, r=[PR[b1 + c], o.R_rstd, R_cols], w=[R_kvn[t]])
        STT(o.u2, PB[b3], col("kgC"), sink[tb][:, tc0:tc0 + 512], ALU.mult, ALU.mult, r=[PR[b3], R_tk[tb], R_cols], w=[o.R_u2])
        yield
        TT("pool", RT[:, ksl], o.u1, o.u2, ALU.add, r=[o.R_u1, o.R_u2], w=[R_RT[t]])
        for j in range(4):
            kt = t * 4 + j
            kk = slice(kt * 128, (kt + 1) * 128)

            def tk(e, kk=kk):
                inst = None
                for half in range(2):
                    for c in range(2):
                        inst = e.matmul(PB[b1 + half], lhsT=kvnT[:, c, kk], rhs=wkv[:, c, half * 512:(half + 1) * 512], start=(c == 0), stop=(c == 1))
                return inst
            pe(tk, r=[R_kvn[t], R_wkv], w=[PR[b1], PR[b2]])
            yield
            AC(o.sqK, psum[:, b1 * 512:(b1 + 2) * 512], AF.Square, r=[PR[b1], PR[b2]], w=[o.R_sqK])
            yield
            sch.add("dve", (lambda e, o_=ssqk[:, kt * 8:(kt + 1) * 8], i_=o.sqK.rearrange("p (a b) -> p a b", a=8): e.tensor_reduce(out=o_, in_=i_, axis=AX.X, op=ALU.add)),
                    r=[o.R_sqK], w=[R_ssqk])
            TS("dve", ssqk[:, kt * 8:(kt + 1) * 8], ssqk[:, kt * 8:(kt + 1) * 8], PB[b0][:, j:j + 1], None, ALU.add, None, r=[PR[b0], R_ssqk], w=[R_ssqk])
            yield

    interleave([gen_a1(t) for t in range(16)])
    lnsc = mem.alloc(1, F32, "a1")
    R_lnsc = Res("lnsc")
    dve(lambda e: e.memset(lnsc, -0.5 * math.log(192.0)), w=[R_lnsc])
    AC(ssqk, ssqk, AF.Ln, r=[R_ssqk, R_eps], w=[R_ssqk], scale=1.0 / 192.0, bias=epsc)
    AC(rstdk, ssqk, AF.Exp, r=[R_ssqk, R_lnsc], w=[R_rstdk], scale=-0.5, bias=lnsc)
    tap("kvnT0", kvnT[:, 0, :], R_kvn[15])
    tap("RT", RT, R_RT[15])
    tap("rstdk", rstdk, R_rstdk)

    if stop_after == "A1":
        dve(lambda e: e.memset(SL[0].lnv, 0.0), w=[SL[0].R_lnv])
        for k in range(8):
            dma("sp", out_d[k * 128:(k + 1) * 128, 0:512], SL[0].lnv, r=[SL[0].R_lnv], is_out=True)
        with nc.Block() as block:
            sch.emit(block)
        es.close()
        return nc

    sch.phase = 2
    sch.barrier()
    mem.free("a1", "tab", "slk")
    cqn = mem.alloc(3 * NQ, BF16, "cqn").rearrange("p (k n) -> p k n", k=3)
    R_cqn = [Res("cqn%d" % j) for j in range(5)]

    def gen_a2a(j):
        d0, n, s0 = QTILES[j]
        o = SL[j % 2]
        b0, b1, b2, b3 = o.b
        yield from gen_norm1(xq3, s0, n, o, o.xn, o.R_xn)
        for c in range(3):
            mm(PB[b1 + c][:, 0:n], [(wA[:, k, c * 128:(c + 1) * 128], o.xn[:, k, 0:n]) for k in range(8)], r=[R_wA, o.R_xn], w=[PR[b1 + c]])
        yield
        for c in range(3):
            AC(o.sqc[:, c, 0:n], PB[b1 + c][:, 0:n], AF.Square, r=[PR[b1 + c]], w=[o.R_sqc])
        yield
        mm(PB[b0][:, 0:n], [(ones, o.sqc[:, c, 0:n]) for c in range(3)], r=[R_ones, o.R_sqc], w=[PR[b0]])
        yield
        AC(o.lnv[:, 0:n], PB[b0][:, 0:n], AF.Ln, r=[PR[b0], R_eps], w=[o.R_lnv], scale=1.0 / 384, bias=epsc)
        AC(o.rstd[:, 0:n], o.lnv[:, 0:n], AF.Exp, r=[o.R_lnv], w=[o.R_rstd], scale=-0.5)
        yield
        for c in range(3):
            STT(cqn[:, c, d0:d0 + n], PB[b1 + c][:, 0:n], col("qng", c), o.rstd[:, 0:n], ALU.mult, ALU.mult, r=[PR[b1 + c], o.R_rstd, R_cols], w=[R_cqn[j]])
        yield

    interleave([gen_a2a(j) for j in range(5)])
    sch.barrier()
    mem.free("slxs", "wA")
    QN = mem.alloc(H * NQ, BF16, "Q").rearrange("p (h n) -> p h n", h=H)
    QR = mem.alloc(H * NQ, BF16, "Q").rearrange("p (h n) -> p h n", h=H)
    wuq = mem.alloc(3 * H * 384, BF16, "wuq").rearrange("p (k n) -> p k n", k=3)
    R_wuq = Res("wuq")
    wload(wuq, wuq_d, 3, R_wuq)
    cosq = mem.alloc(NQ, F32, "tabq")
    sinq = mem.alloc(NQ, F32, "tabq")
    R_tq = Res("tabq")
    scrq = {k: mem.alloc(512, I32 if k in ("pi", "ki") else F32, "tabq") for k in ("pi", "ang", "ki", "r1")}
    for (d0_, n_, _s) in QTILES:
        rope_tables(posq, d0_, n_, cosq[:, d0_:d0_ + n_], sinq[:, d0_:d0_ + n_], scrq, R_tq, R_scr)

    def gen_a2b(j, h, o):
        d0, n, s0 = QTILES[j]
        cs = slice(d0, d0 + n)
        b0, b1, b2, b3 = o.b
        for i3 in range(3):
            mm(PB[b1 + i3][:, 0:n], [(wuq[:, c, h * 384 + i3 * 128:h * 384 + (i3 + 1) * 128], cqn[:, c, cs]) for c in range(3)],
               r=[R_wuq, R_cqn[j]], w=[PR[b1 + i3]])
        yield
        for i3 in range(2):
            AC(o.sqc[:, i3, 0:n], PB[b1 + i3][:, 0:n], AF.Square, r=[PR[b1 + i3]], w=[o.R_sqc])
        yield
        mm(PB[b0][:, 0:n], [(ones, o.sqc[:, 0, 0:n]), (ones, o.sqc[:, 1, 0:n])], r=[R_ones, o.R_sqc], w=[PR[b0]])
        STT(o.u1[:, 0:n], PB[b2][:, 0:n], col("qgB"), cosq[:, cs], ALU.mult, ALU.mult, r=[PR[b2], R_tq, R_cols], w=[o.R_u1])
        STT(o.u2[:, 0:n], PB[b3][:, 0:n], col("qgC"), sinq[:, cs], ALU.mult, ALU.mult, r=[PR[b3], R_tq, R_cols], w=[o.R_u2])
        yield
        AC(o.lnv[:, 0:n], PB[b0][:, 0:n], AF.Ln, r=[PR[b0], R_eps], w=[o.R_lnv], scale=1.0 / 192, bias=epsc)
        AC(o.rstd[:, 0:n], o.lnv[:, 0:n], AF.Exp, r=[o.R_lnv], w=[o.R_rstd], scale=-0.5)
        TT("pool", o.u1[:, 0:n], o.u1[:, 0:n], o.u2[:, 0:n], ALU.add, r=[o.R_u1, o.R_u2], w=[o.R_u1])
        yield
        STT(QN[:, h, cs], PB[b1][:, 0:n], gqk, o.rstd[:, 0:n], ALU.mult, ALU.mult, r=[PR[b1], o.R_rstd, R_gqk], w=[R_Q[h][j]])
        TT("dve", QR[:, h, cs], o.u1[:, 0:n], o.rstd[:, 0:n], ALU.mult, r=[o.R_u1, o.R_rstd], w=[R_Q[h][j]])
        yield

    gl = []
    ii = 0
    for j in range(5):
        for h in range(H):
            gl.append(gen_a2b(j, h, SL[ii % 2]))
            ii += 1
    interleave(gl)
    tap("QN0", QN[:, 0, :], R_Q[0][4])
    tap("QR0", QR[:, 0, :], R_Q[0][4])
    tap("QN7", QN[:, 7, :], R_Q[7][4])

    if stop_after == "A":
        dve(lambda e: e.memset(SL[0].lnv, 0.0), w=[SL[0].R_lnv])
        for k in range(8):
            dma("sp", out_d[k * 128:(k + 1) * 128, 0:512], SL[0].lnv, r=[SL[0].R_lnv], is_out=True)
        with nc.Block() as block:
            sch.emit(block)
        es.close()
        return nc

    sch.phase = 3
    sch.barrier()
    mem.free("wuq", "cqn", "tabq", "sl")
    OT = mem.alloc(H * NQ, BF16, "OT").rearrange("p (h n) -> p h n", h=H)
    R_OT = [Res("OT%d" % h) for h in range(H)]
    KT = mem.alloc(S, BF16, "B")
    VV = mem.alloc(S, BF16, "B")
    R_KT = [Res("KT%d" % g) for g in range(16)]
    R_VV = [Res("VV%d" % g) for g in range(16)]
    NP = 4
    Pb = [mem.alloc(512, BF16, "B") for _ in range(NP)]
    R_P = [Res("P%d" % i) for i in range(NP)]
    rinv = mem.alloc(512, F32, "B")
    accS = mem.alloc(512, F32, "B")
    R_accS = Res("accS")
    accP = mem.alloc(512, F32, "B")
    R_accP = Res("accP")
    onesf = mem.alloc(128, F32, "B")
    R_onesf = Res("onesf")
    dve(lambda e: e.memset(onesf, 1.0), w=[R_onesf])
    R_rinv = Res("rinv")
    SB = [0, 1, 2]

    def gen_kv(h, g):
        gs = slice(g * 512, (g + 1) * 512)
        mm(PB[6], [(wkv[:, c, h * 128:(h + 1) * 128], kvnT[:, c, gs]) for c in range(2)], r=[R_wkv, R_kvn[g]], w=[PR[6]])
        dve(lambda e: e.tensor_copy(out=KT[:, gs], in_=PB[6]), r=[PR[6]], w=[R_KT[g]])

        def vg(e):
            inst = None
            for j in range(4):
                kk = slice(g * 512 + j * 128, g * 512 + (j + 1) * 128)
                for c in range(2):
                    inst = e.matmul(PB[7][:, j * 128:(j + 1) * 128], lhsT=kvnT[:, c, kk], rhs=wkv[:, c, 1024 + h * 128:1024 + (h + 1) * 128],
                                    start=(c == 0), stop=(c == 1))
            return inst
        pe(vg, r=[R_wkv, R_kvn[g]], w=[PR[7]])
        dve(lambda e: e.tensor_copy(out=VV[:, gs], in_=PB[7]), r=[PR[7]], w=[R_VV[g]])

    for g in range(16):
        gen_kv(0, g)
    BT = [(i * 410, 410) for i in range(5)]
    heads = range(H) if stop_after != "B1" else range(1)
    wC_pref = False
    for h in heads:
        if h == H - 1 and stop_after is None:
            mem.free("kv1")
            wCb = mem.alloc(8 * 2052, BF16, "wC")[:, 0:8 * 2048].rearrange("p (k n) -> p k n", k=8)
            R_wC = Res("wC")
            wload(wCb, wC_d[:, 0:2048], 8, R_wC, extra_w=R_kvn + [R_wkv])
            wC_pref = True
        for qi, (d0, n) in enumerate(BT):
            cs = slice(d0, d0 + n)
            ob = 3 if RS_MODE == "hybrid" else 3 + (qi % 2)
            last_q = (qi == len(BT) - 1)

            def qk(kt):
                sb = SB[kt % 3]
                ks = slice(kt * 128, (kt + 1) * 128)
                mm(PB[sb][:, 0:n], [(KT[:, ks], QN[:, h, cs]), (RT[:, ks], QR[:, h, cs])],
                   r=[R_KT[kt // 4], R_RT[kt // 4]] + R_Q[h], w=[PR[sb]])

            qk(0)
            qk(1)
            for kt in range(64):
                sb = SB[kt % 3]
                pb = kt % NP
                if kt + 2 < 64:
                    qk(kt + 2)
                act(lambda e, sb=sb, pb=pb, kt=kt, n=n, h=h: e.activation(out=Pb[pb][:, 0:n], in_=PB[sb][:, 0:n], func=AF.Exp, scale=rstdk[:, kt * 8 + h:kt * 8 + h + 1]),
                    r=[PR[sb], R_rstdk], w=[R_P[pb]])
                ks = slice(kt * 128, (kt + 1) * 128)
                pe(lambda e, ks=ks, pb=pb, kt=kt, ob=ob, n=n: e.matmul(PB[ob][:, 0:n], lhsT=VV[:, ks], rhs=Pb[pb][:, 0:n], start=(kt == 0), stop=(kt == 63)),
                   r=[R_VV[kt // 4], R_P[pb]], w=[PR[ob]])
                if RS_MODE == "hybrid":
                    if kt % RS_EVERY == 0 and RS_POOL:
                        if kt == 0:
                            sch.add("pool", (lambda e, o_=accP[:, 0:n], i_=Pb[pb][:, 0:n]: e.tensor_copy(out=o_, in_=i_)), r=[R_P[pb]], w=[R_accP])
                        else:
                            TT("pool", accP[:, 0:n], Pb[pb][:, 0:n], accP[:, 0:n], ALU.add, r=[R_P[pb], R_accP], w=[R_accP])
                    elif kt % RS_EVERY == 0:
                        pe(lambda e, pb=pb, kt=kt, n=n: e.matmul(PB[5][:, 0:n], lhsT=ones, rhs=Pb[pb][:, 0:n], start=(kt == 0), stop=False),
                           r=[R_ones, R_P[pb]], w=[PR[5]])
                    elif kt == 1:
                        sch.add("dve", (lambda e, o_=PB[4][:, 0:n], i_=Pb[pb][:, 0:n]: e.tensor_copy(out=o_, in_=i_)), r=[R_P[pb]], w=[PR[4]])
                    else:
                        TT("dve", PB[4][:, 0:n], Pb[pb][:, 0:n], PB[4][:, 0:n], ALU.add, r=[R_P[pb], PR[4]], w=[PR[4]])
                else:
                    pe(lambda e, pb=pb, kt=kt, n=n: e.matmul(PB[5][:, 0:n], lhsT=ones, rhs=Pb[pb][:, 0:n], start=(kt == 0), stop=(kt == 63)),
                       r=[R_ones, R_P[pb]], w=[PR[5]])
                if last_q and h + 1 < H and kt % 4 == 3 and stop_after != "B1":
                    gen_kv(h + 1, kt // 4)
            if RS_MODE == "hybrid":
                sch.add("dve", (lambda e, o_=accS[:, 0:n], i_=PB[4][:, 0:n]: e.tensor_copy(out=o_, in_=i_)), r=[PR[4]], w=[R_accS])
                if RS_POOL:
                    sch.add("pe", (lambda e, o_=PB[5][:, 0:n], r_=accP[:, 0:n]: e.matmul(o_, lhsT=onesf, rhs=r_, start=True, stop=False)), r=[R_onesf, R_accP], w=[PR[5]])
                sch.add("pe", (lambda e, o_=PB[5][:, 0:n], r_=accS[:, 0:n]: e.matmul(o_, lhsT=onesf, rhs=r_, start=False, stop=True)), r=[R_onesf, R_accS], w=[PR[5]])
            dve(lambda e, n=n: e.reciprocal(out=rinv[:, 0:n], in_=PB[5][:, 0:n]), r=[PR[5]], w=[R_rinv])
            dve(lambda e, ob=ob, h=h, cs=cs, n=n: e.tensor_tensor(out=OT[:, h, cs], in0=PB[ob][:, 0:n], in1=rinv[:, 0:n], op=ALU.mult), r=[PR[ob], R_rinv], w=[R_OT[h]])
    tap("OT0", OT[:, 0, :], R_OT[0])
    if stop_after in ("B", "B1"):
        dve(lambda e: e.memset(rinv, 0.0), r=[R_rinv], w=[R_rinv])
        for k in range(8):
            dma("sp", out_d[k * 128:(k + 1) * 128, 0:512], rinv, r=[R_rinv], is_out=True)
        with nc.Block() as block:
            sch.emit(block)
        es.close()
        return nc

    sch.phase = 4
    sch.barrier()
    mem.free("Q", "B", "kv2")
    if not wC_pref:
        mem.free("kv1")
    XT = [(0, 512), (512, 512), (1024, 512), (1536, 512), (2048, 256)]
    xnq = mem.alloc(8 * NXQ, BF16, "xnq").rearrange("p (k n) -> p k n", k=8)
    R_xnq = [Res("xnq%d" % i) for i in range(5)]
    if not wC_pref:
        wCb = mem.alloc(8 * 2052, BF16, "wC")[:, 0:8 * 2048].rearrange("p (k n) -> p k n", k=8)
        R_wC = Res("wC")
        wload(wCb, wC_d[:, 0:2048], 8, R_wC)
    SLc = make_slots("c", [[0, 1, 2, 3], [4, 5, 6, 7]])

    def gen_xnq(i):
        c0, n = XT[i]
        yield from gen_norm1(xq3, c0, n, SLc[i % 2], xnq[:, :, c0:c0 + n], R_xnq[i])
    interleave([gen_xnq(i) for i in range(5)])
    sch.barrier()
    mem.free("cxs", "c")

    def xsrc(j):
        d0, n, s0 = QTILES[j]
        return xnq[:, :, s0:s0 + n], R_xnq[4 if j == 4 else j]

    wsT = mem.alloc(1024, BF16, "c2")
    bsr = mem.alloc(1024, BF16, "c2")
    vg_b = mem.alloc(1024, F32, "c2")
    vb_b = mem.alloc(1024, F32, "c2")
    R_c2 = Res("c2consts")
    dma("pool", wsT, wsT_d, w=[R_c2], max_dma_last_dim=4096)
    dma("pool", bsr[0:1, :], bs_d, w=[R_c2], max_dma_last_dim=4096)
    dma("sp", vg_b, vlng_d.partition_broadcast(128), w=[R_c2])
    dma("sp", vb_b, vlnb_d.partition_broadcast(128), w=[R_c2])
    mhalf = mem.alloc(1, F32, "c2")
    R_mh = Res("mhalf")
    dve(lambda e: e.memset(mhalf, -0.5), w=[R_mh])
    yT = mem.alloc(8 * NQ, BF16, "yT").rearrange("p (k n) -> p k n", k=8)
    R_yT = [Res("yT%d" % j) for j in range(5)]
    uT = [mem.alloc(8 * 512, F32, "c2").rearrange("p (k n) -> p k n", k=8) for _ in range(2)]
    R_uT = [Res("uT0"), Res("uT1")]

    class VS:
        pass
    VSL = []
    NVS = 4
    vsq_sh = mem.alloc(1024, BF16, "c2")
    R_vsq_sh = Res("vsq")
    for i in range(NVS):
        o = VS()
        o.vgl = mem.alloc(1024, F32, "c2")
        o.R_vgl = Res("vgl%d" % i)
        o.vsq = vsq_sh
        o.R_vsq = R_vsq_sh
        o.vln = mem.alloc(1024, BF16, "c2")
        o.R_vln = Res("vln%d" % i)
        o.st = mem.alloc(8, F32, "c2")
        o.R_st = Res("st%d" % i)
        o.vb = (2, 3) if i % 2 == 0 else (4, 5)
        o.sb = (6, 7) if i % 2 == 0 else (0, 1)
        VSL.append(o)

    def u_block(j, ub):
        d0, n, s0 = QTILES[j]
        src, rs = xsrc(j)
        for f in range(8):
            pb = f % 2
            mm(PB[pb][:, 0:n], [(wCb[:, k, f * 128:(f + 1) * 128], src[:, k, :]) for k in range(8)], r=[R_wC, rs], w=[PR[pb]])
            AC(uT[ub][:, f, 0:n], PB[pb][:, 0:n], AF.Gelu_apprx_tanh, r=[PR[pb]], w=[R_uT[ub]])

    def gen_v(c0, rs, o, fin):
        v0, v1 = o.vb
        st = o.st

        def vm(e):
            inst = None
            for half in range(2):
                for k in range(8):
                    inst = e.matmul(PB[v0 + half], lhsT=xnq[:, k, c0:c0 + 128], rhs=wCb[:, k, 1024 + half * 512:1024 + (half + 1) * 512], start=(k == 0), stop=(k == 7))
            return inst
        pe(vm, r=[R_wC, rs], w=[PR[v0], PR[v1]])
        AC(o.vgl, psum[:, v0 * 512:(v0 + 2) * 512], AF.Gelu_apprx_tanh, r=[PR[v0], PR[v1]], w=[o.R_vgl, o.R_st], accum_out=st[:, 0:1])
        yield
        STT(o.vsq, o.vgl, 1.0, o.vgl, ALU.mult, ALU.mult, r=[o.R_vgl, o.R_st], w=[o.R_vsq, o.R_st], accum=st[:, 1:2])
        yield
        TS("dve", st[:, 2:3], st[:, 0:1], 1.0 / 1024, None, ALU.mult, None, r=[o.R_st], w=[o.R_st])
        TT("dve", st[:, 6:7], st[:, 2:3], st[:, 2:3], ALU.mult, r=[o.R_st], w=[o.R_st])
        STT(st[:, 3:4], st[:, 1:2], 1.0 / 1024, st[:, 6:7], ALU.mult, ALU.subtract, r=[o.R_st], w=[o.R_st])
        TS("dve", st[:, 3:4], st[:, 3:4], EPS, None, ALU.add, None, r=[o.R_st], w=[o.R_st])
        yield
        TT("pool", st[:, 4:5], st[:, 3:4], mhalf, ALU.pow, r=[o.R_st, R_mh], w=[o.R_st])
        yield
        STT(st[:, 5:6], st[:, 2:3], -1.0, st[:, 4:5], ALU.mult, ALU.mult, r=[o.R_st], w=[o.R_st])
        TS("dve", o.vgl, o.vgl, st[:, 4:5], st[:, 5:6], ALU.mult, ALU.add, r=[o.R_st, o.R_vgl], w=[o.R_vgl])
        yield
        TT("pool", o.vgl, o.vgl, vg_b, ALU.mult, r=[o.R_vgl, R_c2], w=[o.R_vgl])
        TT("pool", o.vln, o.vgl, vb_b, ALU.add, r=[o.R_vgl, R_c2], w=[o.R_vln])
        yield
        if fin is None:
            return
        do_v2(o, fin)
        yield

    def do_v2(o, fin):
        s0_, s1_ = o.sb

        def sp(e):
            inst = None
            for g in range(8):
                oo = PB[(s0_, s1_)[g // 4]][:, (g % 4) * 128:(g % 4 + 1) * 128]
                e.matmul(oo, lhsT=o.vln[:, g * 128:(g + 1) * 128], rhs=wsT[:, g * 128:(g + 1) * 128], start=True, stop=False)
                inst = e.matmul(oo, lhsT=ones[0:1, :], rhs=bsr[0:1, g * 128:(g + 1) * 128], start=False, stop=True)
            return inst
        pe(sp, r=[o.R_vln, R_c2, R_ones], w=[PR[s0_], PR[s1_]])
        fin(o)

    def fin_main(j, c, ub):
        c0 = j * 512 + c * 128

        def f(o):
            for hh in range(2):
                TT("dve", yT[:, hh * 4:(hh + 1) * 4, c0:c0 + 128], PB[o.sb[hh]].rearrange("p (a b) -> p a b", a=4),
                   uT[ub][:, hh * 4:(hh + 1) * 4, c * 128:(c + 1) * 128], ALU.mult, r=[PR[o.sb[hh]], R_uT[ub]], w=[R_yT[j]])
        return f

    def fin_halo(side):
        pc = 127 if side == 0 else 0

        def f(o):
            for hh in range(2):
                TT("dve", yT[:, hh * 4:(hh + 1) * 4, 2048 + side:2049 + side], PB[o.sb[hh]].rearrange("p (a b) -> p a b", a=4)[:, :, pc:pc + 1],
                   uT[0][:, hh * 4:(hh + 1) * 4, side:side + 1], ALU.mult, r=[PR[o.sb[hh]], R_uT[0]], w=[R_yT[4]])
        return f

    u_block(0, 0)
    for j in range(4):
        ub = j % 2
        interleave([gen_v(j * 512 + c * 128, R_xnq[j], VSL[c], None) for c in range(4)], width=4)
        if j + 1 < 4:
            u_block(j + 1, (j + 1) % 2)
        else:
            u_block(4, 0)
        for c in range(4):
            do_v2(VSL[c], fin_main(j, c, ub))
    interleave([gen_v(2048 + side * 128, R_xnq[4], VSL[side], None) for side in range(2)], width=2)
    for side in range(2):
        do_v2(VSL[side], fin_halo(side))
    tap("yT0", yT[:, 0, :], R_yT[4])
    sch.barrier()
    mem.free("c2")
    mem.free("wC")
    wout = mem.alloc(8 * D, BF16, "wout").rearrange("p (k n) -> p k n", k=8)
    R_wout = Res("wout")
    mg = mem.alloc(8 * NQ, BF16, "mg").rearrange("p (k n) -> p k n", k=8)
    R_mg = [Res("mg%d" % j) for j in range(5)]
    NW3 = 2
    w3b = [[mem.alloc(8 * 128, BF16, "w3").rearrange("p (k n) -> p k n", k=8) for _ in range(4)] for _ in range(NW3)]
    R_w3 = [Res("w3_%d" % i) for i in range(NW3)]
    wao3 = wao_d.rearrange("(k p) n -> p k n", p=128)
    wbo3 = wbo_d.rearrange("(k p) n -> p k n", p=128)
    wC3 = wC_d.rearrange("(k p) n -> p k n", p=128)

    def load_w3(f):
        if f < 8:
            bb = f % NW3
            fs = slice(f * 128, (f + 1) * 128)
            dma("pool", w3b[bb][0], wao3[:, :, fs], w=[R_w3[bb]], max_dma_last_dim=4096)
            dma("pool", w3b[bb][1], wC3[:, :, 2048 + f * 128:2048 + (f + 1) * 128], w=[R_w3[bb]], max_dma_last_dim=4096)
            dma("pool", w3b[bb][2], wbo3[:, :, fs], w=[R_w3[bb]], max_dma_last_dim=4096)
            dma("pool", w3b[bb][3], wC3[:, :, 3072 + f * 128:3072 + (f + 1) * 128], w=[R_w3[bb]], max_dma_last_dim=4096)

    class MS:
        pass
    MSL = []
    for i in range(2):
        o = MS()
        o.b = [0, 1, 2, 3] if i == 0 else [4, 5, 6, 7]
        o.sA = mem.alloc(512, F32, "c3")
        o.sB = mem.alloc(512, F32, "c3")
        o.m1 = mem.alloc(512, F32, "c3")
        o.m2 = mem.alloc(512, F32, "c3")
        o.R_sA, o.R_sB, o.R_m1, o.R_m2 = Res("sA%d" % i), Res("sB%d" % i), Res("m1%d" % i), Res("m2%d" % i)
        MSL.append(o)

    def gen_mg(f, j, o):
        d0, n, s0 = QTILES[j]
        src, rs = xsrc(j)
        cs = slice(d0, d0 + n)
        bb = f % NW3
        w = w3b[bb]
        b0, b1, b2, b3 = o.b
        mm(PB[b0][:, 0:n], [(w[0][:, k, :], yT[:, k, cs]) for k in range(8)], r=[R_w3[bb], R_yT[j]], w=[PR[b0]])
        mm(PB[b1][:, 0:n], [(w[1][:, k, :], src[:, k, :]) for k in range(8)], r=[R_w3[bb], rs], w=[PR[b1]])
        mm(PB[b2][:, 0:n], [(w[2][:, k, :], OT[:, k, cs]) for k in range(8)], r=[R_w3[bb]] + R_OT, w=[PR[b2]])
        mm(PB[b3][:, 0:n], [(w[3][:, k, :], src[:, k, :]) for k in range(8)], r=[R_w3[bb], rs], w=[PR[b3]])
        yield
        AC(o.sA[:, 0:n], PB[b1][:, 0:n], AF.Sigmoid, r=[PR[b1]], w=[o.R_sA])
        AC(o.sB[:, 0:n], PB[b3][:, 0:n], AF.Sigmoid, r=[PR[b3]], w=[o.R_sB])
        yield
        TT("dve", o.m1[:, 0:n], PB[b0][:, 0:n], o.sA[:, 0:n], ALU.mult, r=[PR[b0], o.R_sA], w=[o.R_m1])
        TT("dve", o.m2[:, 0:n], PB[b2][:, 0:n], o.sB[:, 0:n], ALU.mult, r=[PR[b2], o.R_sB], w=[o.R_m2])
        yield
        TT("pool", mg[:, f, cs], o.m1[:, 0:n], o.m2[:, 0:n], ALU.add, r=[o.R_m1, o.R_m2], w=[R_mg[j]])
        yield

    load_w3(0)
    wload(wout, wout_d, 8, R_wout)
    gi = 0
    for f in range(8):
        load_w3(f + 1)
        gl = []
        for j in range(5):
            gl.append(gen_mg(f, j, MSL[gi % 2]))
            gi += 1
        interleave(gl)
    sch.barrier()
    mem.free("w3", "c3", "OT", "yT", "xnq")
    sch.phase = 5
    hT = mem.alloc(8 * NQ, F32, "hT").rearrange("p (k n) -> p k n", k=8)
    R_hT = [Res("hT%d" % j) for j in range(5)]
    hn = mem.alloc(8 * NQ, BF16, "hn").rearrange("p (k n) -> p k n", k=8)
    R_hn = Res("hn")

    class HS:
        pass
    HSL = []
    for i in range(2):
        o = HS()
        o.b = (4, 5, 0) if i == 0 else (6, 7, 1)
        o.xs = mem.alloc(8 * 512, F32, "xs").rearrange("p (k n) -> p k n", k=8)
        o.R_xs = Res("xs3_%d" % i)
        o.sq = mem.alloc(8 * 512, BF16, "d0").rearrange("p (k n) -> p k n", k=8)
        o.R_sq = Res("sqd%d" % i)
        o.lnv = mem.alloc(512, F32, "d0")
        o.R_lnv = Res("lnvd%d" % i)
        HSL.append(o)

    def gen_h(j):
        d0, n, s0 = QTILES[j]
        o = HSL[j % 2]
        cs = slice(d0, d0 + n)
        dma("sp", o.xs[:, :, 0:n], xq3[:, :, s0:s0 + n], w=[o.R_xs])
        yield
        for f in range(8):
            pb = o.b[f % 2]
            mm(PB[pb][:, 0:n], [(wout[:, k, f * 128:(f + 1) * 128], mg[:, k, cs]) for k in range(8)], r=[R_wout, R_mg[j]], w=[PR[pb]])
            TT("dve", hT[:, f, cs], PB[pb][:, 0:n], o.xs[:, f, 0:n], ALU.add, r=[PR[pb], o.R_xs], w=[R_hT[j]])
            if f % 2 == 1:
                yield
        for k in range(8):
            if k % 2 == 0:
                TT("pool", o.sq[:, k, 0:n], hT[:, k, cs], hT[:, k, cs], ALU.mult, r=[R_hT[j]], w=[o.R_sq])
            else:
                AC(o.sq[:, k, 0:n], hT[:, k, cs], AF.Square, r=[R_hT[j]], w=[o.R_sq])
        yield
        b0 = o.b[2]
        mm(PB[b0][:, 0:n], [(ones, o.sq[:, k, 0:n]) for k in range(8)], r=[R_ones, o.R_sq], w=[PR[b0]])
        yield
        AC(o.lnv[:, 0:n], PB[b0][:, 0:n], AF.Ln, r=[PR[b0], R_eps], w=[o.R_lnv], scale=1.0 / D, bias=epsc)
        AC(PB[b0][:, 0:n], o.lnv[:, 0:n], AF.Exp, r=[o.R_lnv], w=[PR[b0]], scale=-0.5)
        yield
        for k in range(8):
            STT(hn[:, k, cs], hT[:, k, cs], col("n2g", k), PB[b0][:, 0:n], ALU.mult, ALU.mult, r=[R_hT[j], PR[b0], R_cols], w=[R_hn])
        yield

    interleave([gen_h(j) for j in range(5)])
    tap("hT0", hT[:, 0, :], R_hT[4])
    cwm = mem.alloc(2 * 44, F32, "dm")
    R_cwm = Res("cwm")
    o0, _ = COLS["cw0"]
    o2, _ = COLS["cw2"]
    dve(lambda e: e.tensor_scalar(out=cwm[:, 0:44], in0=cols[:, o0:o0 + 44], scalar1=col("hmask", 0), scalar2=None, op0=ALU.mult), r=[R_cols], w=[R_cwm])
    dve(lambda e: e.tensor_scalar(out=cwm[:, 44:88], in0=cols[:, o2:o2 + 44], scalar1=col("hmask", 1), scalar2=None, op0=ALU.mult), r=[R_cols], w=[R_cwm])
    sch.barrier()
    mem.free("d0", "wout", "mg", "xs")
    actT = mem.alloc(NPAIR * 1024, BF16, "actT").rearrange("p (k n) -> p k n", k=NPAIR)
    R_actT = Res("actT")
    NWB = 4
    wupb = [mem.alloc(8 * 256, BF16, "wup").rearrange("p (k n) -> p k n", k=8) for _ in range(NWB)]
    R_wup = [Res("wup%d" % i) for i in range(NWB)]
    NDB = 4
    wdnb = [mem.alloc(NPAIR * 128, BF16, "wdn") for _ in range(NDB)]
    R_wdn = [Res("wdn%d" % i) for i in range(NDB)]
    acc4 = [mem.alloc(1024, F32, "acc") for _ in range(4)]
    R_acc4 = [Res("acc%d" % i) for i in range(4)]
    obuf = [mem.alloc(512, F32, "acc") for _ in range(2)]
    R_ob = [Res("ob0"), Res("ob1")]
    steps = [(hf, j) for hf in range(2) for j in range(NPAIR)]

    def load_wup(si):
        if si < len(steps):
            _, j = steps[si]
            dma("pool", wupb[si % NWB], wup_d[j].rearrange("(k p) n -> p k n", p=128), w=[R_wup[si % NWB]], max_dma_last_dim=4096)

    def load_wdn(di):
        if di < 16:
            dma("pool", wdnb[di % NDB], wdn_d[di % 8], w=[R_wdn[di % NDB]], max_dma_last_dim=8192)

    load_wup(0)
    load_wup(1)
    oi = 0
    for si, (hf, j) in enumerate(steps):
        load_wup(si + 2)
        if j == NPAIR - 4:
            load_wdn(hf * 8)
            load_wdn(hf * 8 + 1)
        wb = si % NWB
        t0 = hf * 1024
        acc = acc4[2 * (si % 2):2 * (si % 2) + 2]
        R_acc = R_acc4[2 * (si % 2):2 * (si % 2) + 2]
        if hf == 0:
            hal = hn[:, :, 1024:2049:1024]
            li, ri = 1, 0
        else:
            hal = hn[:, :, 1023:2050:1026]
            li, ri = 0, 1
        for vg in range(2):
            cc = j + 22 * vg
            base = 2 * vg * 512
            up = psum[:, base:base + 1024]
            for t2 in range(2):
                mm(PB[2 * vg + t2], [(wupb[wb][:, k, vg * 128:(vg + 1) * 128], hn[:, k, t0 + t2 * 512:t0 + (t2 + 1) * 512]) for k in range(8)],
                   r=[R_wup[wb], R_hn], w=[PR[2 * vg + t2]])
            hp = PB[4 + vg]
            mm(hp[:, 0:2], [(wupb[wb][:, k, vg * 128:(vg + 1) * 128], hal[:, k, :]) for k in range(8)], r=[R_wup[wb], R_hn], w=[PR[4 + vg]])
            a = acc[vg]
            prs = [PR[2 * vg], PR[2 * vg + 1]]
            sch.add("act", (lambda e, o=a, i=up, sc=col("cw1", cc), bi=col("cb", cc): e.activation(out=o, in_=i, func=AF.Identity, scale=sc, bias=bi)),
                    r=prs + [R_cols], w=[R_acc[vg]])
            sch.add("dve", (lambda e, o=a[:, 1:1024], i0=up[:, 0:1023], sc=col("cw0", cc), i1=a[:, 1:1024]: e.scalar_tensor_tensor(out=o, in0=i0, scalar=sc, in1=i1, op0=ALU.mult, op1=ALU.add)),
                    r=prs + [R_cols, R_acc[vg]], w=[R_acc[vg]])
            sch.add("dve", (lambda e, o=a[:, 0:1023], i0=up[:, 1:1024], sc=col("cw2", cc), i1=a[:, 0:1023]: e.scalar_tensor_tensor(out=o, in0=i0, scalar=sc, in1=i1, op0=ALU.mult, op1=ALU.add)),
                    r=prs + [R_cols, R_acc[vg]], w=[R_acc[vg]])
            lsc = cwm[:, cc:cc + 1] if hf == 0 else col("cw0", cc)
            rsc = col("cw2", cc) if hf == 0 else cwm[:, 44 + cc:45 + cc]
            sch.add("dve", (lambda e, o=a[:, 0:1], i0=hp[:, li:li + 1], sc=lsc, i1=a[:, 0:1]: e.scalar_tensor_tensor(out=o, in0=i0, scalar=sc, in1=i1, op0=ALU.mult, op1=ALU.add)),
                    r=[PR[4 + vg], R_cols, R_cwm, R_acc[vg]], w=[R_acc[vg]])
            sch.add("dve", (lambda e, o=a[:, 1023:1024], i0=hp[:, ri:ri + 1], sc=rsc, i1=a[:, 1023:1024]: e.scalar_tensor_tensor(out=o, in0=i0, scalar=sc, in1=i1, op0=ALU.mult, op1=ALU.add)),
                    r=[PR[4 + vg], R_cols, R_cwm, R_acc[vg]], w=[R_acc[vg]])
        sch.add("act", (lambda e, o=acc[1]: e.activation(out=o, in_=o, func=AF.Silu)), r=[R_acc[1]], w=[R_acc[1]])
        sch.add("pool" if PROD_POOL else "dve", (lambda e, o=actT[:, j, :], i0=acc[0], i1=acc[1]: e.tensor_tensor(out=o, in0=i0, in1=i1, op=ALU.mult)), r=[R_acc[0], R_acc[1]], w=[R_actT])
        if j == NPAIR - 1:
            for f in range(8):
                di = hf * 8 + f
                if f + 2 < 8:
                    load_wdn(di + 2)
                db = di % NDB
                for t2 in range(2):
                    pb = 6 + t2
                    ob = oi % 2
                    oi += 1
                    mm(PB[pb], [(wdnb[db][:, jj * 128:(jj + 1) * 128], actT[:, jj, t2 * 512:(t2 + 1) * 512]) for jj in range(NPAIR)], r=[R_wdn[db], R_actT], w=[PR[pb]])
                    c0 = t0 + t2 * 512
                    sch.add("dve", (lambda e, o=obuf[ob], i0=PB[pb], i1=hT[:, f, c0:c0 + 512]: e.tensor_tensor(out=o, in0=i0, in1=i1, op=ALU.add)),
                            r=[PR[pb]] + R_hT, w=[R_ob[ob]])
                    dma("sp", out_d[f * 128:(f + 1) * 128, c0:c0 + 512], obuf[ob], r=[R_ob[ob]], is_out=True)
    with nc.Block() as block:
        sch.emit(block)
    es.close()
    return nc


def _bf(x):
    return np.ascontiguousarray(x, dtype=np.float32)


def prep_inputs(inputs):
    x = np.asarray(inputs["x"], np.float32)
    pos = np.asarray(inputs["positions"], np.int32)
    g = {k: np.asarray(v) for k, v in inputs.items()}
    w_in = g["w_in"]
    u_c, v_c = w_in[:, 0:1024], w_in[:, 1024:2048]
    cq_c, ckv_c, kr_c = w_in[:, 2048:2432], w_in[:, 2432:2688], w_in[:, 2688:2752]
    gA_c, gB_c = w_in[:, 2752:3776], w_in[:, 3776:4800]
    z64 = np.zeros((D, 64), np.float32)
    kr_perm = np.concatenate([kr_c[:, 32:64], kr_c[:, 0:32]], axis=1)
    wA = np.concatenate([cq_c, ckv_c, kr_c, z64, kr_perm, z64], axis=1)
    wuq = g["w_uq"].reshape(384, H, 192)
    z = np.zeros((384, 64), np.float32)
    parts = []
    for h in range(H):
        r = wuq[:, h, 128:192]
        parts += [wuq[:, h, 0:128], r, z, np.concatenate([r[:, 32:64], r[:, 0:32]], axis=1), z]
    wuq_l = np.concatenate(parts, axis=1)
    wukv = g["w_ukv"].reshape(256, H, 256)
    wkv = np.concatenate([wukv[:, :, 0:128].reshape(256, 1024), wukv[:, :, 128:256].reshape(256, 1024)], axis=1)
    wC = np.concatenate([u_c, v_c, gA_c, gB_c], axis=1)
    wup = g["w_up"]
    wup_l = np.stack([np.concatenate([wup[:, j * 128:(j + 1) * 128], wup[:, DFF + j * 128:DFF + (j + 1) * 128]], axis=1) for j in range(NPAIR)])
    wsT = np.transpose(g["w_s"], (2, 0, 1)).reshape(128, 1024)
    cols = np.zeros((128, NCOLS), np.float32)

    def put(name, arr):
        o, w_ = COLS[name]
        cols[:arr.shape[0], o:o + w_] = arr
    put("n1g", g["norm1_g"].reshape(8, 128).T)
    put("qng", g["q_norm_g"].reshape(3, 128).T)
    put("kvng", g["kv_norm_g"].reshape(2, 128).T)
    qh, kh = g["q_head_g"], g["k_head_g"]
    put("qhg_n", qh[0:128, None])
    put("khg_n", kh[0:128, None])
    put("qgB", qh[128:192, None])
    put("qgC", np.concatenate([qh[160:192], qh[128:160]])[:, None])
    put("kgB", kh[128:192, None])
    put("kgC", np.concatenate([kh[160:192], kh[128:160]])[:, None])
    put("n2g", g["norm2_g"].reshape(8, 128).T)
    cw, cb = g["conv_w"], g["conv_b"]
    put("cw0", cw[0].reshape(44, 128).T)
    put("cw1", cw[1].reshape(44, 128).T)
    put("cw2", cw[2].reshape(44, 128).T)
    put("cb", cb.reshape(44, 128).T)
    invf = (np.float32(10000.0) ** (-np.arange(0, 64, 2, dtype=np.float32) / np.float32(64))).astype(np.float32)
    put("invf", np.tile(invf, 4)[:, None])
    sgn = np.ones(128, np.float32)
    sgn[0:32] = -1.0
    put("sgn", sgn[:, None])
    shared = dict(wA=wA, wuq=wuq_l, wkv=wkv, wC=wC, wao=g["w_a_o"], wbo=g["w_b_o"], wout=g["w_out"], wup=wup_l, wdn=np.ascontiguousarray(g["w_down"].reshape(NPAIR, 128, 8, 128).transpose(2, 1, 0, 3)).reshape(8, 128, NPAIR * 128),
                  wsT=wsT, bs=g["b_s"].reshape(1, 1024), vlng=g["v_ln_g"].reshape(1, 1024), vlnb=g["v_ln_b"].reshape(1, 1024))
    shared = {k: _bf(v) for k, v in shared.items()}
    xT = [np.ascontiguousarray(x[b].T) for b in range(2)]
    in_maps = []
    for c in range(8):
        b, qi = c // 4, c % 4
        t0 = qi * TQ
        xq = np.zeros((D, NXQ), np.float32)
        xq[:, 0:TQ] = xT[b][:, t0:t0 + TQ]
        pq = np.zeros((1, NQ), np.int32)
        pq[0, 0:TQ] = pos[b, t0:t0 + TQ]
        cc = cols.copy()
        o, _ = COLS["hmask"]
        if qi > 0:
            xq[:, TQ:TQ + 128] = xT[b][:, t0 - 128:t0]
            pq[0, TQ] = pos[b, t0 - 1]
            cc[:, o] = 1.0
        if qi < 3:
            xq[:, TQ + 128:TQ + 256] = xT[b][:, t0 + TQ:t0 + TQ + 128]
            pq[0, TQ + 1] = pos[b, t0 + TQ]
            cc[:, o + 1] = 1.0
        m = dict(shared)
        m.update(xkv=xT[b], xq=xq, posk=np.ascontiguousarray(pos[b:b + 1]), posq=pq, cols=cc)
        in_maps.append(m)
    return in_maps


_NC = None


def kernel(**inputs):
    global _NC
    in_maps = prep_inputs(inputs)
    if _NC is None:
        _NC = build()
    res = run_bass_kernel_spmd(_NC, in_maps, core_ids=list(range(8)))
    out = np.zeros((2, S, D), np.float32)
    for c in range(8):
        b, qi = c // 4, c % 4
        out[b, qi * TQ:(qi + 1) * TQ, :] = res.results[c]["out"].T
    return out
```
